# Optimizing a Trainium2 kernel written in Bass

```python
import math
import jax, jax.numpy as jnp
from jax import lax
import numpy as np

D_MODEL = 2048
BATCH = 4
SEQ = 8192
DEPTH = 1

CHUNK = 64
GLA_HEADS = 4
GLA_DK = D_MODEL // 2 // GLA_HEADS
GLA_DV = D_MODEL // GLA_HEADS
GLA_RANK = 16
GLA_TAU = 16.0
GMLP_BLOCK = 128
GMLP_GROUPS = 4
GMLP_DG = D_MODEL // GMLP_GROUPS
D_FF = ((8 * D_MODEL + 3 * 256 - 1) // (3 * 256)) * 256
PLE_DIM = 256
ALPHA = (2 * DEPTH) ** 0.25
BETA = (8 * DEPTH) ** -0.25
LN_EPS = 1e-5

SPLIT_SIZES = (
    GLA_HEADS * GLA_DK,
    GLA_HEADS * GLA_DK,
    GLA_HEADS * GLA_DV,
    GLA_HEADS * GLA_DV,
    GLA_HEADS * GLA_RANK,
    D_MODEL,
    D_MODEL,
    D_MODEL,
    D_MODEL,
)
D_IN = sum(SPLIT_SIZES)

kernel_name = "hybrid_gla_gmlp_deepnorm_block"


def _split_points():
    pts, acc = [], 0
    for s in SPLIT_SIZES[:-1]:
        acc += s
        pts.append(acc)
    return pts


def layer_norm(x, g, b):
    xf = x.astype(jnp.float32)
    mu = jnp.mean(xf, axis=-1, keepdims=True)
    var = jnp.mean(jnp.square(xf - mu), axis=-1, keepdims=True)
    return ((xf - mu) * lax.rsqrt(var + LN_EPS) * g + b).astype(x.dtype)


def rms_norm(x, g):
    xf = x.astype(jnp.float32)
    return (xf * lax.rsqrt(jnp.mean(jnp.square(xf), axis=-1, keepdims=True) + LN_EPS) * g).astype(x.dtype)


def gla_chunk_causal(q, k, v, log_a):
    bsz, s, h, dk = q.shape
    dv = v.shape[-1]
    nc = s // CHUNK
    f32 = jnp.float32
    q = q.astype(f32).reshape(bsz, nc, CHUNK, h, dk) * (dk ** -0.5)
    k = k.astype(f32).reshape(bsz, nc, CHUNK, h, dk)
    v = v.astype(f32).reshape(bsz, nc, CHUNK, h, dv)
    b = jnp.cumsum(log_a.astype(f32).reshape(bsz, nc, CHUNK, h, dk), axis=2)
    b_end = b[:, :, -1:]
    k_t = k * jnp.exp(b_end - b)
    q_t = q * jnp.exp(b_end)
    scores = jnp.einsum('bnihd,bnjhd->bnhij', q, k_t)
    o_intra = jnp.einsum('bnhij,bnjhv->bnihv', scores, v)

    def step(state, xs):
        qc, kc, vc, dc = xs
        o = jnp.einsum('blhd,bhdv->blhv', qc, state)
        state = jnp.exp(dc)[..., None] * state + jnp.einsum('blhd,blhv->bhdv', kc, vc)
        return state, o

    xs = (jnp.moveaxis(q_t, 1, 0), jnp.moveaxis(k_t, 1, 0), jnp.moveaxis(v, 1, 0),
          jnp.moveaxis(b_end[:, :, 0], 1, 0))
    state0 = jnp.zeros((bsz, h, dk, dv), f32)
    _, o_inter = lax.scan(step, state0, xs)
    o = o_intra + jnp.moveaxis(o_inter, 0, 1)
    return o.reshape(bsz, s, h, dv)


def gmlp_spatial_gate(u, z, ln_g, ln_b, w_s, b_s):
    bsz, s, d = u.shape
    nb = s // GMLP_BLOCK
    zn = layer_norm(z.reshape(bsz, nb, GMLP_BLOCK, GMLP_GROUPS, GMLP_DG), ln_g, ln_b)
    pos_chunk = jnp.arange(GMLP_BLOCK) // CHUNK
    mask = pos_chunk[:, None] >= pos_chunk[None, :]
    w = jnp.where(mask[None], w_s, jnp.zeros_like(w_s))
    sg = jnp.einsum('gij,bnjgc->bnigc', w, zn) + b_s.T[None, None, :, :, None]
    return u * sg.reshape(bsz, s, d)


def setup_inputs(seed: int = 0) -> dict:
    key = jax.random.key(seed)
    ks = jax.random.split(key, 24)
    f32 = jnp.float32
    nrm = lambda k, shape: jax.random.normal(k, shape, f32)
    L = DEPTH
    return {
        "x": nrm(ks[0], (BATCH, SEQ, D_MODEL)),
        "p": nrm(ks[1], (DEPTH, BATCH, SEQ, PLE_DIM)),
        "ln0_g": 1.0 + 0.01 * nrm(ks[2], (D_MODEL,)),
        "ln0_b": 0.01 * nrm(ks[3], (D_MODEL,)),
        "w_in": nrm(ks[4], (L, D_MODEL, D_IN)) * D_MODEL ** -0.5,
        "b_in": 0.01 * nrm(ks[5], (L, D_IN)),
        "w_f2": nrm(ks[6], (L, GLA_HEADS, GLA_RANK, GLA_DK)) * GLA_RANK ** -0.5,
        "b_f2": 0.1 * nrm(ks[7], (L, GLA_HEADS, GLA_DK)),
        "gla_norm_g": 1.0 + 0.01 * nrm(ks[8], (L, GLA_HEADS, GLA_DV)),
        "gmlp_ln_g": 1.0 + 0.01 * nrm(ks[9], (L, GMLP_GROUPS, GMLP_DG)),
        "gmlp_ln_b": 0.01 * nrm(ks[10], (L, GMLP_GROUPS, GMLP_DG)),
        "w_s": nrm(ks[11], (L, GMLP_GROUPS, GMLP_BLOCK, GMLP_BLOCK)) * GMLP_BLOCK ** -0.5,
        "b_s": 1.0 + 0.01 * nrm(ks[12], (L, GMLP_GROUPS, GMLP_BLOCK)),
        "w_o": nrm(ks[13], (L, D_MODEL, D_MODEL)) * (D_MODEL ** -0.5 * BETA),
        "ln1_g": 1.0 + 0.01 * nrm(ks[14], (L, D_MODEL)),
        "ln1_b": 0.01 * nrm(ks[15], (L, D_MODEL)),
        "w_gu": nrm(ks[16], (L, D_MODEL, 2 * D_FF)) * D_MODEL ** -0.5,
        "w_down": nrm(ks[17], (L, D_FF, D_MODEL)) * (D_FF ** -0.5 * BETA),
        "w_pg": nrm(ks[18], (L, D_MODEL, D_MODEL)) * D_MODEL ** -0.5,
        "b_pg": 0.01 * nrm(ks[19], (L, D_MODEL)),
        "w_pu": nrm(ks[20], (L, PLE_DIM, D_MODEL)) * (PLE_DIM ** -0.5 * BETA),
        "ln2_g": 1.0 + 0.01 * nrm(ks[21], (L, D_MODEL)),
        "ln2_b": 0.01 * nrm(ks[22], (L, D_MODEL)),
    }


def reference(x, p, ln0_g, ln0_b, w_in, b_in, w_f2, b_f2, gla_norm_g, gmlp_ln_g, gmlp_ln_b,
              w_s, b_s, w_o, ln1_g, ln1_b, w_gu, w_down, w_pg, b_pg, w_pu, ln2_g, ln2_b):
    bsz, s, d = x.shape
    pts = _split_points()
    h = layer_norm(x, ln0_g, ln0_b)
    for i in range(DEPTH):
        proj = h @ w_in[i] + b_in[i]
        q, k, v, og, fg, u, z, ga, gb = jnp.split(proj, pts, axis=-1)
        fg = fg.reshape(bsz, s, GLA_HEADS, GLA_RANK)
        f_logit = jnp.einsum('bshr,hrd->bshd', fg, w_f2[i]) + b_f2[i]
        log_a = jax.nn.log_sigmoid(f_logit.astype(jnp.float32)) / GLA_TAU
        o = gla_chunk_causal(q.reshape(bsz, s, GLA_HEADS, GLA_DK),
                             k.reshape(bsz, s, GLA_HEADS, GLA_DK),
                             v.reshape(bsz, s, GLA_HEADS, GLA_DV), log_a)
        y_a = rms_norm(o, gla_norm_g[i]).reshape(bsz, s, d) * jax.nn.silu(og)
        y_b = gmlp_spatial_gate(jax.nn.gelu(u, approximate=False), jax.nn.gelu(z, approximate=False),
                                gmlp_ln_g[i], gmlp_ln_b[i], w_s[i], b_s[i])
        m = jax.nn.sigmoid(ga) * y_a + jax.nn.sigmoid(gb) * y_b
        h = layer_norm(ALPHA * h + m @ w_o[i], ln1_g[i], ln1_b[i])
        gt, up = jnp.split(h @ w_gu[i], 2, axis=-1)
        ffn = (jax.nn.silu(gt) * up) @ w_down[i]
        ple = jax.nn.sigmoid(h @ w_pg[i] + b_pg[i]) * (p[i] @ w_pu[i])
        h = layer_norm(ALPHA * h + ffn + ple, ln2_g[i], ln2_b[i])
    return h
```

```python
import numpy as np
import os
DISJ = os.environ.get('K_DISJ', '1') == '1'
from contextlib import ExitStack
from collections import deque

import concourse.bass as bass
import concourse.mybir as mybir
from concourse.bass_utils import run_bass_kernel_spmd

F32 = mybir.dt.float32
BF16 = mybir.dt.bfloat16
AF = mybir.ActivationFunctionType
ALU = mybir.AluOpType

D = 2048
SEQ = 8192
NB = 4
HEADS = 4
DK = 256
DV = 512
RANK = 16
DFF = 5632
NFF = DFF // 128
PLE = 256
DEPTH = 1
ALPHA = (2 * DEPTH) ** 0.25
EPS = 1e-5
T = 512
TOK_PER_CORE = 4096
NT = TOK_PER_CORE // T
KC = D // 128

O_Q, O_K, O_V, O_OG, O_FG, O_U, O_Z, O_GA, O_GB = 0, 1024, 2048, 4096, 6144, 6208, 8256, 10304, 12352


def _tile_k(w, col_lists):
    K = w.shape[0]
    out = []
    for cols in col_lists:
        sub = w[:, cols]
        sub = sub.reshape(K // 128, 128, len(cols)).transpose(1, 0, 2)
        out.append(sub.reshape(128, -1))
    return np.ascontiguousarray(np.stack(out, 0))


def _win_tiles():
    tiles = []
    for h in range(HEADS):
        tiles.append((("q", h), O_Q + h * 256))
        tiles.append((("k", h), O_K + h * 256))
        for j in range(2):
            tiles.append((("v", h, j), O_V + h * 512 + j * 256))
            tiles.append((("og", h, j), O_OG + h * 512 + j * 256))
            tiles.append((("ga", h, j), O_GA + h * 512 + j * 256))
    for g in range(4):
        for j in range(2):
            tiles.append((("u", g, j), O_U + g * 512 + j * 256))
            tiles.append((("gb", g, j), O_GB + g * 512 + j * 256))
            tiles.append((("z", g, j), O_Z + g * 512 + j * 256))
    return tiles


WIN_TILES = _win_tiles()
WIN_IDX = {name: i for i, (name, _) in enumerate(WIN_TILES)}
WIN_COL = {name: c for name, c in WIN_TILES}
AROW_NAMES = [n for n, _ in WIN_TILES if n[0] in ("v", "z")] + [("pg", j) for j in range(8)]
AROW_IDX = {n: i for i, n in enumerate(AROW_NAMES)}
BCOL_NAMES = []
for _n, _c in WIN_TILES:
    if _n[0] not in ("v", "z"):
        BCOL_NAMES += [(_n, 0), (_n, 1)]
BCOL_IDX = {n: i for i, n in enumerate(BCOL_NAMES)}
NBCOL = len(BCOL_NAMES)
WD_PIECES = [(0, 9), (9, 9), (18, 9), (27, 9), (36, 8)]


def _prep_shared(inp):
    f = np.float32
    w_in = np.asarray(inp["w_in"], f)[0]
    b_in = np.asarray(inp["b_in"], f)[0]
    sh = {}
    sh["win_f"] = _tile_k(w_in, [np.arange(c, c + 256) for _, c in WIN_TILES])
    sh["fg_f"] = _tile_k(w_in, [np.arange(O_FG, O_FG + 64)])[0]
    w_o = np.asarray(inp["w_o"], f)[0]
    sh["wo_f"] = _tile_k(w_o, [np.arange(j * 256, j * 256 + 256) for j in range(8)])
    w_gu = np.asarray(inp["w_gu"], f)[0]
    sh["wgu_f"] = _tile_k(w_gu, [np.concatenate([np.arange(i * 128, i * 128 + 128),
                                                  np.arange(DFF + i * 128, DFF + i * 128 + 128)])
                                 for i in range(NFF)])
    w_pg = np.asarray(inp["w_pg"], f)[0]
    w_pu = np.asarray(inp["w_pu"], f)[0]
    a = _tile_k(w_pg, [np.arange(j * 256, j * 256 + 256) for j in range(8)])
    b = _tile_k(w_pu, [np.arange(j * 256, j * 256 + 256) for j in range(8)])
    sh["wpg_f"] = np.ascontiguousarray(np.concatenate([a, b], axis=2))
    w_d = np.asarray(inp["w_down"], f)[0]
    wd = np.zeros((20, 128, 9 * 512), f)
    for cb in range(4):
        for pi, (k0, nk) in enumerate(WD_PIECES):
            sub = w_d[k0 * 128:(k0 + nk) * 128, cb * 512:(cb + 1) * 512]
            sub = sub.reshape(nk, 128, 512).transpose(1, 0, 2).reshape(128, nk * 512)
            wd[cb * 5 + pi, :, :nk * 512] = sub
    sh["wd_f"] = wd
    brow = np.zeros((len(AROW_NAMES), 256), f)
    b_pg = np.asarray(inp["b_pg"], f)[0]
    for n, i in AROW_IDX.items():
        if n[0] == "pg":
            brow[i] = b_pg[n[1] * 256:(n[1] + 1) * 256]
        else:
            brow[i] = b_in[WIN_COL[n]:WIN_COL[n] + 256]
    sh["brow_f"] = brow
    bcol = np.zeros((128, NBCOL + 1), f)
    for (n, m), i in BCOL_IDX.items():
        c0 = WIN_COL[n] + m * 128
        bcol[:, i] = b_in[c0:c0 + 128]
    bcol[:64, NBCOL] = b_in[O_FG:O_FG + 64]
    sh["bcol"] = bcol

    def cols(v, n):
        return np.ascontiguousarray(np.asarray(v, f).reshape(n, 128).T)

    small = np.concatenate([
        cols(inp["ln0_g"], 16), cols(inp["ln0_b"], 16),
        cols(np.asarray(inp["ln1_g"])[0], 16), cols(np.asarray(inp["ln1_b"])[0], 16),
        cols(np.asarray(inp["gla_norm_g"])[0], 16),
        cols(np.asarray(inp["b_f2"])[0], 8),
    ], axis=1)
    sh["small"] = np.ascontiguousarray(small)
    bc = np.stack([
        np.stack([np.asarray(inp["gmlp_ln_g"], f)[0].reshape(-1), np.asarray(inp["gmlp_ln_b"], f)[0].reshape(-1)]),
        np.stack([np.asarray(inp["ln0_g"], f), np.asarray(inp["ln0_b"], f)]),
        np.stack([np.asarray(inp["ln1_g"], f)[0], np.asarray(inp["ln1_b"], f)[0]]),
        np.stack([np.asarray(inp["ln2_g"], f)[0], np.asarray(inp["ln2_b"], f)[0]]),
    ])
    sh["bc_rows"] = np.ascontiguousarray(bc)
    wf2 = np.asarray(inp["w_f2"], f)[0]
    bd = np.zeros((64, 1024), f)
    for h in range(HEADS):
        bd[h * 16:(h + 1) * 16, h * 256:(h + 1) * 256] = wf2[h]
    sh["wf2bd"] = bd
    ws = np.asarray(inp["w_s"], f)[0]
    sh["wsT"] = np.ascontiguousarray(ws.transpose(2, 0, 1).reshape(128, 512))
    sh["bs_row"] = np.ascontiguousarray(np.asarray(inp["b_s"], f)[0].reshape(1, 512))
    sh["ident"] = np.eye(128, dtype=f)
    m = np.ones((128, 512), f)
    m[:, 0::64] = 0.0
    sh["scanmask"] = m
    return sh


class Sem:
    def __init__(self, h, name):
        self.h, self.name, self.count = h, name, 0


class Buf:
    def __init__(self, name, lo=None, hi=None):
        self.name, self.lo, self.hi = name, lo, hi
        self.w = {}
        self.r = {}
        self.alias = []
        self.valid = True


class V:
    def __init__(self, ap, bufs):
        self.ap = ap
        self.bufs = bufs if isinstance(bufs, (list, tuple)) else [bufs]

    def __getitem__(self, k):
        return V(self.ap[k], self.bufs)

    def re(self, s, **kw):
        return V(self.ap.rearrange(s, **kw), self.bufs)

    def bc(self, shape):
        return V(self.ap.broadcast_to(shape), self.bufs)

    def only(self, *idx):
        return V(self.ap, [self.bufs[i] for i in idx])


class Eng:
    def __init__(self, name, sem):
        self.name, self.sem = name, sem
        self.seen = {}
        self.ops = []


class Prog:
    def __init__(self, nc, stack):
        self.nc, self.stack = nc, stack
        self.eng = {}
        for n in ("pe", "act", "dve", "pool", "sp"):
            s = Sem(stack.enter_context(nc.semaphore("s_" + n)), n)
            self.eng[n] = Eng(n, s)
        self.nsem = 0

    def new_sem(self, name):
        self.nsem += 1
        return Sem(self.stack.enter_context(self.nc.semaphore("d_%s_%d" % (name, self.nsem))), name)

    def _deps(self, E, reads, writes, accum, disjoint=False):
        deps = {}

        def need(st):
            if st is None:
                return
            s, c = st
            if deps.get(s, 0) < c:
                deps[s] = c

        for b in reads:
            for st in b.w.items():
                need(st)
            for a in b.alias:
                for st in a.w.items():
                    need(st)
        for b in writes:
            if not disjoint:
                for st in b.w.items():
                    need(st)
            for s, c in b.r.items():
                need((s, c))
            for a in b.alias:
                for st in a.w.items():
                    need(st)
                for s, c in a.r.items():
                    need((s, c))
        for s, c in deps.items():
            if s is E.sem and E.name in ("pe", "sp"):
                continue
            if E.seen.get(s, 0) >= c:
                continue
            E.seen[s] = c
            E.ops.append(("wait", s.h, c))

    def emit(self, en, fn, reads=(), writes=(), accum=False, disjoint=False):
        E = self.eng[en]
        rb = [b for v in reads for b in v.bufs]
        wb = [b for v in writes for b in v.bufs]
        self._deps(E, rb, wb, accum, disjoint)
        E.sem.count += 1
        E.ops.append(("ins", fn, E.sem.h, 1))
        st = (E.sem, E.sem.count)
        for b in rb:
            assert b.valid, "stale read of %s (overwritten through an alias)" % b.name
            if b.r.get(E.sem, 0) < E.sem.count:
                b.r[E.sem] = E.sem.count
        for b in wb:
            if disjoint:
                b.w[st[0]] = st[1]
            else:
                b.w = {st[0]: st[1]}
                b.r = {}
            b.valid = True
            for a in b.alias:
                a.valid = False

    def dma(self, en, out, in_, sem, reads=(), writes=(), **kw):
        E = self.eng[en]
        rb = [b for v in reads for b in v.bufs]
        wb = [b for v in writes for b in v.bufs]
        self._deps(E, rb, wb, False)
        sem.count += 16
        E.ops.append(("ins", lambda e, o=out, i=in_, kw=kw: e.dma_start(out=o, in_=i, **kw), sem.h, 16))
        st = (sem, sem.count)
        for b in rb:
            assert b.valid, "stale read of %s (overwritten through an alias)" % b.name
            b.r[sem] = sem.count
        for b in wb:
            b.w = {st[0]: st[1]}
            b.r = {}
            b.valid = True
            for a in b.alias:
                a.valid = False

    def wait_all(self, en, sems):
        E = self.eng[en]
        for s in sems:
            if s.count > 0:
                E.ops.append(("wait", s.h, s.count))

    def replay(self, en, handle):
        for op in self.eng[en].ops:
            if op[0] == "wait":
                handle.wait_ge(op[1], op[2])
            else:
                ins = op[1](handle)
                ins.then_inc(op[2], op[3])


class Arena:
    def __init__(self, nc, stack, nbytes):
        self.nbytes = nbytes
        self.t = stack.enter_context(nc.sbuf_tensor("arena", [128, nbytes // 2], BF16))
        self.bufs = []

    def alloc(self, name, free, dtype, off, nsub=1, parts=128):
        esz = 4 if dtype == F32 else 2
        n = int(np.prod(free))
        nbytes = n * esz
        assert off % 4 == 0 and off + nbytes <= self.nbytes, (name, off, nbytes, self.nbytes)
        ap = self.t[0:parts, off // 2:(off + nbytes) // 2]
        if dtype == F32:
            ap = ap.bitcast(F32)
        if len(free) == 2:
            ap = ap.rearrange("p (a b) -> p a b", b=free[1])
        elif len(free) == 3:
            ap = ap.rearrange("p (a b c) -> p a b c", b=free[1], c=free[2])
        bufs = []
        sub = nbytes // nsub
        for i in range(nsub):
            b = Buf("%s%d" % (name, i), off + i * sub, off + (i + 1) * sub)
            for o in self.bufs:
                if o.lo < b.hi and b.lo < o.hi:
                    o.alias.append(b)
                    b.alias.append(o)
            self.bufs.append(b)
            bufs.append(b)
        return V(ap, bufs)


def build_program(n_pre=NT, n_main=NT, debug=False):
    nc = bass.Bass("TRN2", target_bir_lowering=False)
    stack = ExitStack()
    P = Prog(nc, stack)

    def din(name, shape, dt=F32):
        return nc.dram_tensor(name, list(shape), dt, kind="ExternalInput").ap()

    def dscr(name, shape, dt=BF16):
        return nc.dram_tensor(name, list(shape), dt, kind="Internal").ap()

    x_main = din("x_main", [n_main * T, D])
    x_prev = din("x_prev", [max(n_pre, 1) * T, D])
    p_in = din("p_in", [n_main * T, PLE])
    flag = din("flag", [128, 1])
    win_f = din("win_f", [len(WIN_TILES), 128, 16 * 256])
    fg_f = din("fg_f", [128, 16 * 64])
    wo_f = din("wo_f", [8, 128, 16 * 256])
    wgu_f = din("wgu_f", [NFF, 128, 16 * 256])
    wpg_f = din("wpg_f", [8, 128, 18 * 256])
    wd_f = din("wd_f", [20, 128, 9 * 512])
    brow_f = din("brow_f", [len(AROW_NAMES), 256])
    bcol_d = din("bcol", [128, NBCOL + 1])
    small_d = din("small", [128, 88])
    bc_rows = din("bc_rows", [4, 2, D])
    wf2bd_d = din("wf2bd", [64, 1024])
    wsT_d = din("wsT", [128, 512])
    bs_row_d = din("bs_row", [1, 512])
    ident_d = din("ident", [128, 128])
    mask_d = din("scanmask", [128, 512])
    out_d = nc.dram_tensor("out", [n_main * T, D], F32, kind="ExternalOutput").ap()

    win_s = dscr("win_s", [len(WIN_TILES), 128, 16 * 256])
    fg_s = dscr("fg_s", [128, 16 * 64])
    wo_s = dscr("wo_s", [8, 128, 16 * 256])
    wgu_s = dscr("wgu_s", [NFF, 128, 16 * 256])
    wpg_s = dscr("wpg_s", [8, 128, 18 * 256])
    wd_s = dscr("wd_s", [20, 128, 9 * 512])
    brow_s = dscr("brow_s", [len(AROW_NAMES), 256])

    KB = 1024
    off = [0]

    def take(nbytes):
        o = off[0]
        off[0] += (nbytes + 31) // 32 * 32
        return o

    ARENA = 207 * KB
    A = Arena(nc, stack, ARENA)
    WSLOT = 18 * 256
    NSLOT = 4
    wslots = [A.alloc("w%d" % i, (WSLOT,), BF16, take(WSLOT * 2)) for i in range(NSLOT)]
    wrows = [A.alloc("wr%d" % i, (256,), BF16, take(512)) for i in range(NSLOT)]
    for i in range(NSLOT):
        wrows[i].bufs = wslots[i].bufs
    wsem = [P.new_sem("w") for _ in range(NSLOT)]
    hT = A.alloc("hT", (KC, T), BF16, take(KC * T * 2))
    S32 = A.alloc("S32", (8, 512), F32, take(8 * 512 * 4), nsub=8)
    Sbf = A.alloc("Sbf", (8, 512), BF16, take(8 * 512 * 2), nsub=8)
    bcs = [A.alloc("bc%d" % i, (2, D), F32, take(2 * D * 4)) for i in range(1)]
    bcsem = [P.new_sem("bc") for _ in range(1)]
    xnA0 = A.alloc("xnA0", (D,), BF16, take(D * 2))
    pT = A.alloc("pT", (2, T), BF16, take(2 * T * 2))
    xsem = [P.new_sem("x") for _ in range(5)]
    psem = P.new_sem("p")
    bcol = A.alloc("bcol", (NBCOL + 1,), F32, take((NBCOL + 1) * 4))
    small = A.alloc("small", (88,), F32, take(88 * 4))
    nbf2 = A.alloc("nbf2", (8,), F32, take(32))
    flagc = A.alloc("flagc", (1,), F32, take(32))
    identb = A.alloc("identb", (128,), BF16, take(256))
    mask = A.alloc("mask", (512,), F32, take(2048))
    wf2b = A.alloc("wf2b", (1024,), BF16, take(2048), parts=64)
    wmT = A.alloc("wmT", (512,), BF16, take(1024))
    ones2 = A.alloc("ones2", (128,), BF16, take(256), parts=2)
    bs2 = A.alloc("bs2", (512,), BF16, take(1024), parts=2)
    mhalf = A.alloc("mhalf", (1,), F32, take(32))
    csem = P.new_sem("const")
    st6 = A.alloc("st6", (4, 4, 6), F32, take(4 * 96), nsub=4)
    mv = A.alloc("mv", (4, 2), F32, take(32), nsub=4)
    ln0s2 = [A.alloc("ln0s%d" % i, (4, 2), F32, take(32), nsub=4) for i in range(2)]
    lns = A.alloc("lns", (4, 2), F32, take(32), nsub=4)
    sq = A.alloc("sq", (2, 4), F32, take(32), nsub=2)
    dec = A.alloc("dec", (8, 8), F32, take(256))
    ovl = take(0)
    assert ovl + 98 * KB <= ARENA, (ovl, ARENA)

    def at(kb):
        return ovl + int(kb * KB)

    qsT = [A.alloc("qsT%d" % i, (2, T), BF16, at(0 + 2 * i)) for i in range(2)]
    ktT = [A.alloc("ktT%d" % i, (2, T), BF16, at(4 + 2 * i)) for i in range(2)]
    kttok = [A.alloc("ktt%d" % i, (4, 256), BF16, at(8 + 2 * i)) for i in range(2)]
    vtok = [A.alloc("vtok%d" % i, (4, 512), BF16, at(12 + 4 * i)) for i in range(2)]
    P1 = [A.alloc("P1_%d" % i, (4, T), BF16, at(20 + 4 * i)) for i in range(2)]
    actb = [A.alloc("actb%d" % i, (T,), BF16, at(56 + i)) for i in range(3)]
    xnb1 = [A.alloc("xnb1_%d" % i, (D,), BF16, at(32 + 4 * i)) for i in range(4)]
    pbuf = A.alloc("pbuf", (4, PLE), F32, at(56))
    pnb = A.alloc("pnb", (4, PLE), BF16, at(60))
    P2 = [A.alloc("P2_%d" % i, (4, T), BF16, at(32 + 4 * i)) for i in range(2)]
    gz = A.alloc("gz", (4, 512), F32, at(40), nsub=4)
    zn = [A.alloc("zn%d" % i, (4, 512), BF16, at(48)) for i in range(1)]
    tmpS = [A.alloc("tmpS%d" % i, (4, 128), F32, at(52 + 2 * i)) for i in range(2)]
    kdec = A.alloc("kdec", (8, T), BF16, at(62), nsub=8)
    tmpE = [A.alloc("tmpE%d" % i, (T,), F32, at(70 + 2 * i)) for i in range(2)]
    tmpB = [A.alloc("tmpB%d" % i, (T,), F32, at(74 + 2 * i)) for i in range(2)]
    fgT = A.alloc("fgT", (T,), BF16, at(78), parts=64)
    ohat = [A.alloc("ohat%d" % i, (512,), BF16, at(79 + i)) for i in range(2)]
    junk = A.alloc("junk", (512,), BF16, at(81))
    mT = A.alloc("mT", (KC, T), BF16, at(82), nsub=KC)
    xbufA = A.alloc("xbA", (D,), F32, at(76))
    xnA = [xnA0] + [A.alloc("xnA%d" % i, (D,), BF16, at(84 + 4 * (i - 1))) for i in range(1, 4)]
    stg = A.alloc("stg", (1024,), F32, at(0))
    resid = A.alloc("resid", (4, D), F32, at(0), nsub=4)
    actT = A.alloc("actT", (NFF, T), BF16, at(32))
    sgt = [A.alloc("sgt%d" % i, (T,), F32, at(76 + 2 * i)) for i in range(2)]
    spg = [A.alloc("spg%d" % i, (256,), F32, at(80 + i)) for i in range(2)]
    ple = [A.alloc("ple%d" % i, (256,), F32, at(82 + i)) for i in range(2)]
    pgt = [A.alloc("pgt%d" % i, (256,), F32, at(84 + i)) for i in range(2)]
    ztmp = [A.alloc("ztmp%d" % i, (256,), F32, at(59 + i)) for i in range(2)]

    banks = []
    for i in range(8):
        t = stack.enter_context(nc.psum_tensor("pb%d" % i, [128, 512], F32))
        banks.append(V(t[:], Buf("pb%d" % i)))

    def bank_bf(b):
        return V(b.ap.bitcast(BF16), b.bufs)

    def mm(out, lhsT, rhs, start, stop):
        P.emit("pe", lambda e, o=out.ap, l=lhsT.ap, r=rhs.ap, s=start, t=stop: e.matmul(o, lhsT=l, rhs=r, start=s, stop=t),
               reads=[lhsT, rhs], writes=[out], accum=not start)

    def tr(out, in_):
        P.emit("pe", lambda e, o=out.ap, i=in_.ap, d=identb.ap: e.transpose(o, i, d[0:i.shape[0], 0:i.shape[0]]),
               reads=[in_, identb], writes=[out], accum=True)

    def act(out, in_, func, bias=None, scale=None, accum_out=None, disjoint=False):
        kw = {}
        rd = [in_]
        if bias is not None:
            if isinstance(bias, V):
                kw["bias"] = bias.ap
                rd.append(bias)
            else:
                kw["bias"] = float(bias)
        if scale is not None:
            if isinstance(scale, V):
                kw["scale"] = scale.ap
                rd.append(scale)
            else:
                kw["scale"] = float(scale)
        wr = [out]
        if accum_out is not None:
            kw["accum_out"] = accum_out.ap
            wr.append(accum_out)
        P.emit("act", lambda e, o=out.ap, i=in_.ap, f=func, kw=kw: e.activation(out=o, in_=i, func=f, **kw),
               reads=rd, writes=wr, disjoint=disjoint)

    def _sc(x, rd):
        if isinstance(x, V):
            rd.append(x)
            return x.ap
        return None if x is None else float(x)

    def ts(en, out, in0, s1, s2, op0, op1=None, disjoint=False):
        rd = [in0]
        a1, a2 = _sc(s1, rd), _sc(s2, rd)
        if op1 is None:
            P.emit(en, lambda e, o=out.ap, i=in0.ap: e.tensor_scalar(out=o, in0=i, scalar1=a1, scalar2=None, op0=op0),
                   reads=rd, writes=[out])
        else:
            P.emit(en, lambda e, o=out.ap, i=in0.ap: e.tensor_scalar(out=o, in0=i, scalar1=a1, scalar2=a2, op0=op0, op1=op1),
                   reads=rd, writes=[out], disjoint=disjoint)

    def stt(out, in0, s, in1, op0, op1):
        rd = [in0, in1]
        a = _sc(s, rd)
        P.emit("dve", lambda e, o=out.ap, i0=in0.ap, i1=in1.ap: e.scalar_tensor_tensor(out=o, in0=i0, scalar=a, in1=i1, op0=op0, op1=op1),
               reads=rd, writes=[out])

    def tt(en, out, in0, in1, op):
        P.emit(en, lambda e, o=out.ap, i0=in0.ap, i1=in1.ap: e.tensor_tensor(out=o, in0=i0, in1=i1, op=op),
               reads=[in0, in1], writes=[out])

    def cp(en, out, in_):
        if en == "act":
            act(out, in_, AF.Copy)
        else:
            P.emit(en, lambda e, o=out.ap, i=in_.ap: e.tensor_copy(out=o, in_=i), reads=[in_], writes=[out])

    def memset(en, out, val):
        P.emit(en, lambda e, o=out.ap: e.memset(o, val), writes=[out])

    MUL, ADD, SUB, POW = ALU.mult, ALU.add, ALU.subtract, ALU.pow

    conv_bufs = {}

    def conv_family(name, dst, src, ntiles, width, order=None, single=()):
        sem = P.new_sem("cv_" + name)
        ch = max(c for c in range(1, 2049) if width % c == 0)
        bufs = {}
        grouped = []
        for i in (order if order is not None else range(ntiles)):
            b = Buf("%s_s%d" % (name, i))
            d_ap = dst[i] if ntiles > 1 or len(dst.shape) == 3 else dst
            s_ap = src[i] if ntiles > 1 or len(src.shape) == 3 else src
            sm = P.new_sem("cv1_" + name) if i in single else sem
            P.dma("pool", d_ap.rearrange("p (a b) -> p a b", b=ch), s_ap.rearrange("p (a b) -> p a b", b=ch), sm,
                  writes=[V(None, b)])
            bufs[i] = b
            if i not in single:
                grouped.append(b)
        for b in grouped:
            b.w = {sem: sem.count}
        conv_bufs[name] = [bufs[i] for i in range(ntiles)]

    def setup():
        cdma = []

        def ld(dst, src):
            P.dma("sp", dst.ap, src, P.new_sem("c"), writes=[dst])

        ld(bcol, bcol_d)
        ld(small, small_d)
        ld(flagc, flag)
        ld(mask, mask_d)
        ld(stg[:, 0:128], ident_d)
        cp("dve", identb, stg[:, 0:128])
        ld(stg[0:64, 0:1024], wf2bd_d)
        cp("dve", wf2b, stg[0:64, 0:1024])
        ld(stg[:, 0:512], wsT_d)
        memset("dve", stg[64:128, 0:512].re("p (g i) -> p g i", i=128)[:, :, 0:64], 0.0)
        cp("dve", wmT, stg[:, 0:512])
        ld(stg[0:1, 0:512], bs_row_d)
        cp("dve", bs2[0:1, :], stg[0:1, 0:512])
        cp("dve", stg[0:1, 512:1024], bs2[0:1, :])
        tt("dve", stg[0:1, 0:512], stg[0:1, 0:512], stg[0:1, 512:1024], SUB)
        cp("dve", junk[0:1, 0:512], stg[0:1, 0:512])
        P.dma("sp", bs2.ap[1:2, :], junk.ap[0:1, 0:512], P.new_sem("c"), reads=[junk], writes=[bs2])
        memset("dve", ones2, 1.0)
        memset("dve", mhalf, -0.5)
        ts("dve", nbf2, small[:, 80:88], -1.0, None, MUL)
        for i in range(8):
            memset("dve", S32.only(i)[:, i, :], 0.0)
            memset("dve", Sbf.only(i)[:, i, :], 0.0)

    wstate = {"n": 0}

    class WLoad:
        pass

    def issue_wload(src_ap, ncols_total, srcbuf, brow_idx=None):
        i = wstate["n"] % NSLOT
        wstate["n"] += 1
        sl = wslots[i]
        P.dma("sp", sl.ap[:, 0:ncols_total], src_ap, wsem[i], reads=[V(None, srcbuf)], writes=[sl])
        if brow_idx is not None:
            rsrc = brow_s[brow_idx:brow_idx + 1, :]
            rsrc = bass.AP(rsrc.tensor, rsrc.offset, [[0, 128], [1, 256]])
            P.dma("sp", wrows[i].ap, rsrc, wsem[i],
                  reads=[V(None, conv_bufs["brow"][0])], writes=[sl])
        w = WLoad()
        w.v = sl
        w.row = wrows[i]
        return w

    bg = deque()

    def pump(n=1):
        for _ in range(n):
            if bg:
                bg.popleft()()

    def drain():
        while bg:
            bg.popleft()()

    def keep(n):
        while len(bg) > n:
            bg.popleft()()

    bcstate = {"n": 0}

    def load_bc(which):
        i = 0
        src = bc_rows[which]
        src_b = bass.AP(src.tensor, src.offset, [[0, 128], [D, 2], [1, D]])
        P.dma("pool", bcs[i].ap, src_b, bcsem[i], writes=[bcs[i]])
        return bcs[i]

    def rstd_from(var_v, out_rstd, tmp):
        ts("pool", tmp, var_v, EPS, None, ADD)
        tt("pool", out_rstd, tmp, mhalf, POW)

    def ln_stats(src, rstd_out, nmr_out, nchunk=4, width=512, tb=0):
        s6 = st6.only(tb)[:, tb, :, :]
        m2 = mv.only(tb)[:, tb, :]
        for c in range(nchunk):
            P.emit("dve", lambda e, o=s6.ap[:, c, :], i=src.ap[:, c * width:(c + 1) * width]: e.bn_stats(out=o, in_=i),
                   reads=[src], writes=[s6])
        P.emit("dve", lambda e, o=m2.ap, i=s6.ap[:, 0:nchunk, :].rearrange("p a b -> p (a b)"): e.bn_aggr(out=o, in_=i),
               reads=[s6], writes=[m2])
        rstd_from(m2[:, 1:2], rstd_out, m2[:, 1:2])
        stt(nmr_out, m2[:, 0:1], -1.0, rstd_out, MUL, MUL)

    xstate = {"n": 0}


    def transposes_to_fm(src_bf, dstT, tb, gcol0, bcol0, ptb):
        for r in range(2):
            ptv = bank_bf(banks[2 + r])
            for j in range(8):
                k = r * 8 + j
                tr(ptv[:, j * 128:(j + 1) * 128], src_bf[:, k * 128:(k + 1) * 128])
            for j in range(8):
                k = r * 8 + j
                eng = "dve" if (r % 2 == 0 or os.environ.get('K_EV') == 'dve') else "act"
                if eng == "dve":
                    ts("dve", dstT[:, k, tb * 128:(tb + 1) * 128], ptv[:, j * 128:(j + 1) * 128],
                       small[:, gcol0 + k:gcol0 + k + 1], small[:, bcol0 + k:bcol0 + k + 1], MUL, ADD, disjoint=DISJ)
                else:
                    act(dstT[:, k, tb * 128:(tb + 1) * 128], ptv[:, j * 128:(j + 1) * 128], AF.Identity,
                        bias=small[:, bcol0 + k:bcol0 + k + 1], scale=small[:, gcol0 + k:gcol0 + k + 1], disjoint=DISJ)

    lnpar = {"n": 0}

    hooks = deque()

    def stage_ln0_pre(xd, t, now=False):
        lnpar["n"] += 1
        l0 = ln0s2[lnpar["n"] % 2]

        def load(tb):
            r0 = t * T + tb * 128
            P.dma("act", xbufA.ap, xd[r0:r0 + 128, :], xsem[0], writes=[xbufA])

        def comp(tb):
            lv = l0.only(tb)[:, tb, :]
            ln_stats(xbufA, lv[:, 0:1], lv[:, 1:2], tb=tb)
            act(xnA[tb], xbufA, AF.Identity, bias=lv[:, 1:2], scale=lv[:, 0:1])

        pieces = [lambda: load(0)]
        for tb in range(4):
            pieces.append(lambda tb=tb: (comp(tb), load(tb + 1) if tb < 3 else None))
        if now:
            for p_ in pieces:
                p_()
        else:
            hooks.extend(pieces)

    def run_hooks():
        while hooks:
            hooks.popleft()()

    def stage_ln0(xd, t):
        run_hooks()
        for tb in range(4):
            transposes_to_fm(xnA[tb], hT, tb, 0, 16, banks[2])
            pump(3)
        drain()

    def cur_ln0s():
        return ln0s2[lnpar["n"] % 2]

    def stage_p(t):
        P.dma("sp", pbuf.ap, p_in[t * T:(t + 1) * T, :].rearrange("(a p) c -> p a c", p=128), psem, writes=[pbuf])
        cp("pool", pnb, pbuf)
        ptv = bank_bf(banks[2])
        for tb in range(4):
            for k2 in range(2):
                tr(ptv[:, (tb * 2 + k2) * 128:(tb * 2 + k2 + 1) * 128], pnb[:, tb, k2 * 128:(k2 + 1) * 128])
        for k2 in range(2):
            cp("dve", pT[:, k2, :].re("p (a b) -> p a b", b=128),
               ptv.re("p (a k b) -> p a k b", k=2, b=128)[:, :, k2, :])

    def proj_B(w, nk, m_off, M, rhsT, bank, N=T, wcols=256):
        wv = w.v.re("p (k c) -> p k c", c=wcols) if not hasattr(w, "re3") else w.re3
        for k in range(nk):
            mm(bank[0:M, 0:N], wv[:, k, m_off:m_off + M], rhsT[:, k, 0:N], k == 0, k == nk - 1)

    def proj_A(w, nk, srcT, tb, bank, ncol=256, bias=False, k_off=0, last=True, first=True):
        wv = w.v.re("p (k c) -> p k c", c=ncol)
        for k in range(nk):
            mm(bank[:, 0:ncol], srcT[:, k, tb * 128:(tb + 1) * 128], wv[:, k_off + k, :], first and k == 0,
               last and (not bias) and k == nk - 1)
        if bias:
            mm(bank[:, 0:ncol], ones2[0:1, :], w.row[0:1, 0:ncol], False, last)

    pbrot = {"n": 0}

    def next_bank(lo=0, n=2):
        b = banks[lo + pbrot["n"] % n]
        pbrot["n"] += 1
        return b

    def stage_fg_decay(w):
        b = next_bank()
        wv = w.v.re("p (k c) -> p k c", c=64)
        for k in range(KC):
            mm(b[0:64, 0:T], wv[:, k, 0:64], hT[:, k, :], k == 0, k == KC - 1)
        act(fgT, b[0:64, 0:T], AF.Identity, bias=bcol[0:64, NBCOL:NBCOL + 1])

    def decay_hooks(delay):
        def partA(d8):
            pm = banks[4 + d8 % 4]
            mm(pm[:, 0:T], wf2b[0:64, d8 * 128:(d8 + 1) * 128], fgT[0:64, :], True, True)
            e, bb = tmpE[d8 % 2], tmpB[d8 % 2]
            act(e, pm[:, 0:T], AF.Exp, bias=nbf2[:, d8:d8 + 1], scale=-1.0)
            act(e, e, AF.Ln, bias=1.0)
            P.emit("dve", lambda en, o=bb.ap, d0=mask.ap, d1=e.ap: en.tensor_tensor_scan(out=o, data0=d0, data1=d1, initial=0.0, op0=MUL, op1=ADD),
                   reads=[mask, e], writes=[bb])

        def partB(d8):
            e, bb = tmpE[d8 % 2], tmpB[d8 % 2]
            b3 = bb.re("p (c i) -> p c i", i=64)
            act(dec[:, d8, :], b3[:, :, 63], AF.Exp, scale=-1.0 / 16.0)
            tt("dve", e.re("p (c i) -> p c i", i=64), b3, b3[:, :, 63:64].bc([128, 8, 64]), SUB)
            act(kdec.only(d8)[:, d8, :], e, AF.Exp, scale=1.0 / 16.0)

        hooks.extend([lambda: None] * delay)
        hooks.append(lambda: (partA(0), partA(1)))
        hooks.append(lambda: (partB(0), partB(1), partA(2), partA(3)))
        hooks.append(lambda: (partB(2), partB(3), partA(4), partA(5)))
        hooks.append(lambda: (partB(4), partB(5), partA(6), partA(7)))
        hooks.append(lambda: (partB(6), partB(7)))

    def stage_q(w, h):
        hb = h % 2
        for m in range(2):
            b = next_bank()
            proj_B(w, KC, m * 128, 128, hT, b)
            ci = BCOL_IDX[(("q", h), m)]
            ts("dve", qsT[hb][:, m, :], b[:, 0:T], bcol[:, ci:ci + 1], 1.0 / 16.0, ADD, MUL)
            pump()

    def stage_k(w, h):
        hb = h % 2
        for m in range(2):
            b = next_bank()
            proj_B(w, KC, m * 128, 128, hT, b)
            ci = BCOL_IDX[(("k", h), m)]
            stt(ktT[hb][:, m, :], b[:, 0:T], bcol[:, ci:ci + 1], kdec.only(h * 2 + m)[:, h * 2 + m, :], ADD, MUL)
            pump()

    def stage_v(w, h, j):
        hb = h % 2
        for tb in range(4):
            b = next_bank()
            proj_A(w, KC, hT, tb, b)
            tt("dve", vtok[hb][:, tb, j * 256:(j + 1) * 256], b[:, 0:256], w.row, ADD)
            pump()

    def gla_tasks(h, with_o):
        hb = h % 2
        ptv = bank_bf(banks[2])

        def k_transposes():
            for tb in range(4):
                for dc in range(2):
                    tr(ptv[:, (tb * 2 + dc) * 128:(tb * 2 + dc + 1) * 128], ktT[hb][:, dc, tb * 128:(tb + 1) * 128])
            cp("dve", kttok[hb].re("p a b -> p (a b)"), ptv[:, 0:1024])

        def dS(c):
            tb, pr = c // 2, c % 2
            rows = slice(pr * 64, pr * 64 + 64)
            for dc in range(2):
                i8 = h * 2 + dc
                pd = banks[4 + dc]
                mm(pd[:, 0:512], kttok[hb][rows, tb, dc * 128:(dc + 1) * 128], vtok[hb][rows, tb, :], True, True)
                sv = S32.only(i8)[:, i8, :]
                stt(sv, sv, dec[:, i8, c:c + 1], pd[:, 0:512], MUL, ADD)
                if with_o:
                    cp("pool", Sbf.only(i8)[:, i8, :], sv)

        def o_mm(c):
            tb, pr = c // 2, c % 2
            rows = slice(pr * 64, pr * 64 + 64)
            po = banks[6 + tb % 2]
            for dc in range(2):
                i8 = h * 2 + dc
                mm(po[rows, 0:512], qsT[hb][:, dc, c * 64:(c + 1) * 64], Sbf.only(i8)[:, i8, :], dc == 0, dc == 1)

        def opost_a(tb):
            po = banks[6 + tb % 2]
            s = sq.only(tb % 2)[:, tb % 2, :]
            act(junk, po[:, 0:512], AF.Square, accum_out=s[:, 0:1])
            ts("pool", s[:, 1:2], s[:, 0:1], 1.0 / 512.0, EPS, MUL, ADD)
            tt("pool", s[:, 2:3], s[:, 1:2], mhalf, POW)
            oh = ohat[tb % 2]
            act(oh, po[:, 0:512], AF.Identity, scale=s[:, 2:3])

        def opost_b(tb):
            oh = ohat[tb % 2]
            for cc in range(4):
                tr(ptv[:, cc * 128:(cc + 1) * 128], oh[:, cc * 128:(cc + 1) * 128])
            for cc in range(4):
                c16 = h * 4 + cc
                stt(mT.only(c16)[:, c16, tb * 128:(tb + 1) * 128], ptv[:, cc * 128:(cc + 1) * 128],
                    small[:, 64 + c16:65 + c16], P1[hb][:, cc, tb * 128:(tb + 1) * 128], MUL, MUL)

        noop = lambda: None
        tasks = [k_transposes, noop]
        if not with_o:
            for c in range(8):
                tasks.append(lambda c=c: dS(c))
            return tasks
        tasks.append(lambda: dS(0))
        for c in range(8):
            def step(c=c):
                o_mm(c)
                if c + 1 < 8:
                    dS(c + 1)
                if c % 2 == 1:
                    opost_a(c // 2)
                if c % 2 == 0 and c >= 2:
                    opost_b(c // 2 - 1)
            tasks.append(step)
        tasks.append(noop)
        tasks.append(lambda: opost_b(3))
        return tasks

    def stage_og_ga(w, h, j, kind):
        hb = h % 2
        for m in range(2):
            b = next_bank()
            proj_B(w, KC, m * 128, 128, hT, b)
            ci = BCOL_IDX[((kind, h, j), m)]
            cc = j * 2 + m
            if kind == "og":
                act(P1[hb][:, cc, :], b[:, 0:T], AF.Silu, bias=bcol[:, ci:ci + 1])
            else:
                a = actb[pbrot["n"] % 3]
                act(a, b[:, 0:T], AF.Sigmoid, bias=bcol[:, ci:ci + 1])
                tt("dve", P1[hb][:, cc, :], P1[hb][:, cc, :], a, MUL)
            pump()

    def stage_u_gb(w, g, j, kind):
        gb_ = g % 2
        for m in range(2):
            b = next_bank()
            proj_B(w, KC, m * 128, 128, hT, b)
            ci = BCOL_IDX[((kind, g, j), m)]
            cc = j * 2 + m
            if kind == "u":
                act(P2[gb_][:, cc, :], b[:, 0:T], AF.Gelu, bias=bcol[:, ci:ci + 1])
            else:
                a = actb[pbrot["n"] % 3]
                act(a, b[:, 0:T], AF.Sigmoid, bias=bcol[:, ci:ci + 1])
                tt("dve", P2[gb_][:, cc, :], P2[gb_][:, cc, :], a, MUL)
            pump()

    def stage_z(w, g, j, bcz):
        gb_ = g % 2
        for tb in range(4):
            b = next_bank()
            proj_A(w, KC, hT, tb, b)
            zt = ztmp[tb % 2]
            tt("dve", zt, b[:, 0:256], w.row, ADD)
            act(gz.only(tb)[:, tb, j * 256:(j + 1) * 256], zt, AF.Gelu)
            pump()
        if j == 1:
            drain()
            for tb in range(4):
                gv = gz.only(tb)[:, tb, :]
                l = lns.only(tb)[:, tb, :]
                ln_stats(gv, l[:, 0:1], l[:, 1:2], nchunk=1, tb=tb)
            for tb in range(4):
                gv = gz.only(tb)[:, tb, :]
                l = lns.only(tb)[:, tb, :]
                act(gv, gv, AF.Identity, bias=l[:, 1:2], scale=l[:, 0:1])
                tt("pool", gv, gv, bcz[:, 0, g * 512:(g + 1) * 512], MUL)
                tt("pool", zn[0][:, tb, :], gv, bcz[:, 1, g * 512:(g + 1) * 512], ADD)

    def spatial_tasks(g):
        gb_ = g % 2
        if True:
            def spatial(tb):
                pg = banks[3]
                for cc in range(4):
                    mm(pg[:, cc * 128:(cc + 1) * 128], zn[0][:, tb, cc * 128:(cc + 1) * 128],
                       wmT[:, g * 128:(g + 1) * 128], True, False)
                    mm(pg[:, cc * 128:(cc + 1) * 128], ones2[0:2, :], bs2[0:2, g * 128:(g + 1) * 128], False, True)
                tsv = tmpS[tb % 2]
                tt("dve", tsv, pg[:, 0:512].re("p (c i) -> p c i", i=128),
                   P2[gb_][:, :, tb * 128:(tb + 1) * 128], MUL)
                idx = list(range(g * 4, g * 4 + 4))
                mv_ = mT.only(*idx)[:, g * 4:(g + 1) * 4, tb * 128:(tb + 1) * 128]
                tt("dve", mv_, mv_, tsv, ADD)

            for tb in range(4):
                bg.append(lambda tb=tb: spatial(tb))
                bg.append(lambda: None)

    def stage_resid_dma(xd, t):
        for tb in range(4):
            r0 = t * T + tb * 128
            rv = resid.only(tb)[:, tb, :]
            P.dma("sp", rv.ap, xd[r0:r0 + 128, :], xsem[1 + tb], writes=[rv])

    def stage_resid_h(xd, t, bc0):
        l0 = cur_ln0s()
        bc0 = bc0()
        for tb in range(4):
            rv = resid.only(tb)[:, tb, :]
            lv = l0.only(tb)[:, tb, :]
            act(rv, rv, AF.Identity, bias=lv[:, 1:2], scale=lv[:, 0:1])
            tt("pool", rv, rv, bc0[:, 0, :], MUL)
            tt("pool", rv, rv, bc0[:, 1, :], ADD)

    def stage_wo(w, j):
        for tb in range(4):
            b = next_bank()
            wv = w.v.re("p (k c) -> p k c", c=256)
            for k in range(KC):
                mm(b[:, 0:256], mT.only(k)[:, k, tb * 128:(tb + 1) * 128], wv[:, k, :], k == 0, k == KC - 1)
            rv = resid.only(tb)[:, tb, j * 256:(j + 1) * 256]
            stt(rv, rv, ALPHA, b[:, 0:256], MUL, ADD)

    def stage_ln1(bc1):
        for tb in range(4):
            rv = resid.only(tb)[:, tb, :]
            l = lns.only(tb)[:, tb, :]
            ln_stats(rv, l[:, 0:1], l[:, 1:2], tb=tb)
        for tb in range(4):
            rv = resid.only(tb)[:, tb, :]
            l = lns.only(tb)[:, tb, :]
            act(xnb1[tb], rv, AF.Identity, bias=l[:, 1:2], scale=l[:, 0:1])
        for tb in range(4):
            transposes_to_fm(xnb1[tb], hT, tb, 32, 48, banks[2])
        bc1 = bc1()
        for tb in range(4):
            rv = resid.only(tb)[:, tb, :]
            l = lns.only(tb)[:, tb, :]
            act(rv, rv, AF.Identity, bias=l[:, 1:2], scale=l[:, 0:1])
            tt("pool", rv, rv, bc1[:, 0, :], MUL)
            tt("pool", rv, rv, bc1[:, 1, :], ADD)

    def stage_gu(w, f):
        bgk = banks[(f % 2) * 2]
        buk = banks[(f % 2) * 2 + 1]
        proj_B(w, KC, 0, 128, hT, bgk)
        proj_B(w, KC, 128, 128, hT, buk)
        s = sgt[f % 2]
        act(s, bgk[:, 0:T], AF.Silu)
        tt("dve", actT[:, f, :], s, buk[:, 0:T], MUL)

    def stage_pg(w, j):
        wv = w.v.re("p (k c) -> p k c", c=256)
        for tb in range(4):
            bp = banks[4 + (tb % 2) * 2]
            bq = banks[5 + (tb % 2) * 2]
            proj_A(w, KC, hT, tb, bp)
            for k2 in range(2):
                mm(bq[:, 0:256], pT[:, k2, tb * 128:(tb + 1) * 128], wv[:, 16 + k2, :], k2 == 0, k2 == 1)
            s = spg[tb % 2]
            pt_ = pgt[tb % 2]
            tt("dve", pt_, bp[:, 0:256], w.row, ADD)
            act(s, pt_, AF.Sigmoid)
            pl = ple[tb % 2]
            tt("dve", pl, s, bq[:, 0:256], MUL)
            rv = resid.only(tb)[:, tb, j * 256:(j + 1) * 256]
            stt(rv, rv, ALPHA, pl, MUL, ADD)

    def stage_wd(w, cb, pi):
        k0, nk = WD_PIECES[pi]
        wv = w.v.re("p (k c) -> p k c", c=512)
        for tb in range(4):
            b = banks[(cb % 2) * 4 + tb]
            for kk in range(nk):
                f = k0 + kk
                mm(b[:, 0:512], actT[:, f, tb * 128:(tb + 1) * 128], wv[:, kk, :], f == 0, f == NFF - 1)
        if pi == len(WD_PIECES) - 1:
            for tb in range(4):
                b = banks[(cb % 2) * 4 + tb]
                rv = resid.only(tb)[:, tb, cb * 512:(cb + 1) * 512]
                tt("dve", rv, b[:, 0:512], rv, ADD)

    osem = [P.new_sem("o") for _ in range(4)]

    def stage_ln2(t, bc2, tbs):
        for tb in tbs:
            rv = resid.only(tb)[:, tb, :]
            l = lns.only(tb)[:, tb, :]
            ln_stats(rv, l[:, 0:1], l[:, 1:2], tb=tb)
        for tb in tbs:
            rv = resid.only(tb)[:, tb, :]
            l = lns.only(tb)[:, tb, :]
            ts("dve", rv, rv, l[:, 0:1], l[:, 1:2], MUL, ADD)
            tt("pool", rv, rv, bc2[:, 0, :], MUL)
            tt("pool", rv, rv, bc2[:, 1, :], ADD)
            r0 = t * T + tb * 128
            P.dma("pool", out_d[r0:r0 + 128, :], rv.ap, osem[tb], reads=[rv])

    def stage_flag():
        for i8 in range(8):
            sv = S32.only(i8)[:, i8, :]
            ts("dve", sv, sv, flagc[:, 0:1], None, MUL)
            cp("pool", Sbf.only(i8)[:, i8, :], sv)

    steps = []

    def wsrc(fam, idx, ncols, brow=None):
        return (fam, idx, ncols, brow)

    def add(src, fn):
        steps.append((src, fn))

    def win(name, brow=False):
        return wsrc("win", WIN_IDX[name], 16 * 256, AROW_IDX[name] if brow else None)

    tiles = [("pre", t) for t in range(n_pre)] + [("main", t) for t in range(n_main)]

    def pre_of(i, now=False):
        kind, t = tiles[i]
        xd = x_prev if kind == "pre" else x_main
        return lambda w: stage_ln0_pre(xd, t, now)

    def head_of(i):
        kind, t = tiles[i]
        if kind == "pre":
            return lambda w: stage_ln0(x_prev, t)
        return lambda w: (stage_ln0(x_main, t), stage_p(t))

    add(None, pre_of(0, True))
    add(None, head_of(0))
    for ti, (kind, t) in enumerate(tiles):
        last_pre = kind == "pre" and (ti + 1 == len(tiles) or tiles[ti + 1][0] != "pre")
        first_main = kind == "main" and (ti == 0 or tiles[ti - 1][0] == "pre")
        if first_main:
            add(None, lambda w: late_conversions())
        add(wsrc("fg", 0, 16 * 64), lambda w, kind=kind: (stage_fg_decay(w), decay_hooks(0 if kind == "pre" else 2)))
        if kind == "pre":
            if ti + 1 < len(tiles):
                add(None, pre_of(ti + 1))
            for h in range(HEADS):
                for j in range(2):
                    add(win(("v", h, j), True), lambda w, h=h, j=j: ((keep(10) if j == 0 else None), stage_v(w, h, j)))
                add(win(("k", h)), lambda w, h=h: stage_k(w, h))
                add(None, lambda w, h=h: bg.extend(gla_tasks(h, False)))
            if ti + 1 < len(tiles):
                add(None, head_of(ti + 1))
            if last_pre:
                add(None, lambda w: (drain(), stage_flag()))
            continue
        st = {}
        for h in range(HEADS):
            for j in range(2):
                add(win(("og", h, j)), lambda w, h=h, j=j: stage_og_ga(w, h, j, "og"))
            for j in range(2):
                add(win(("ga", h, j)), lambda w, h=h, j=j: stage_og_ga(w, h, j, "ga"))
            add(win(("q", h)), lambda w, h=h: stage_q(w, h))
            add(win(("k", h)), lambda w, h=h: stage_k(w, h))
            for j in range(2):
                add(win(("v", h, j), True), lambda w, h=h, j=j: stage_v(w, h, j))
            add(None, lambda w, h=h: (drain(), bg.extend(gla_tasks(h, True))))
        add(None, lambda w, st=st: st.__setitem__("bcz", load_bc(0)))
        for g in range(4):
            if g == 3:
                add(None, lambda w, t=t: (drain(), stage_resid_dma(x_main, t)))
            for j in range(2):
                add(win(("z", g, j), True), lambda w, g=g, j=j, st=st: stage_z(w, g, j, st["bcz"]))
            if g == 3:
                add(None, lambda w, t=t, st=st: stage_resid_h(x_main, t, lambda: load_bc(1)))
            for j in range(2):
                add(win(("u", g, j)), lambda w, g=g, j=j: stage_u_gb(w, g, j, "u"))
            for j in range(2):
                add(win(("gb", g, j)), lambda w, g=g, j=j: stage_u_gb(w, g, j, "gb"))
            add(None, lambda w, g=g: spatial_tasks(g))
        add(None, lambda w: drain())
        for j in range(8):
            add(wsrc("wo", j, 16 * 256), lambda w, j=j: stage_wo(w, j))
        add(None, lambda w, st=st: stage_ln1(lambda: load_bc(2)))
        for f in range(NFF):
            add(wsrc("wgu", f, 16 * 256), lambda w, f=f: stage_gu(w, f))
        add(None, lambda w, st=st: st.__setitem__("bc2", load_bc(3)))
        for j in range(8):
            add(wsrc("wpg", j, 18 * 256, AROW_IDX[("pg", j)]), lambda w, j=j: stage_pg(w, j))
        if ti + 1 < len(tiles):
            add(None, pre_of(ti + 1))
        for cb in range(4):
            for pi in range(len(WD_PIECES)):
                add(wsrc("wd", cb * 5 + pi, 9 * 512), lambda w, cb=cb, pi=pi: stage_wd(w, cb, pi))
        add(None, lambda w, t=t, st=st: stage_ln2(t, st["bc2"], (2,)))
        if ti + 1 < len(tiles):
            add(None, head_of(ti + 1))
        add(None, lambda w, t=t, st=st: stage_ln2(t, st["bc2"], (3, 0, 1)))

    conv_family("fg", fg_s, fg_f, 1, 16 * 64)
    conv_family("brow", brow_s.rearrange("(o a) b -> o a b", o=1), brow_f.rearrange("(o a) b -> o a b", o=1), 1, 256)
    pre_tiles = []
    for h in range(HEADS):
        pre_tiles += [WIN_IDX[("k", h)], WIN_IDX[("v", h, 0)], WIN_IDX[("v", h, 1)]]
    main_order = []
    for h in range(HEADS):
        main_order += [WIN_IDX[(k_, h, j)] for k_ in ("og", "ga") for j in range(2)] + [WIN_IDX[("q", h)]]
    for g in range(4):
        main_order += [WIN_IDX[(k_, g, j)] for k_ in ("z", "u", "gb") for j in range(2)]
    conv_family("win", win_s, win_f, len(WIN_TILES), 16 * 256, order=pre_tiles + main_order,
                single=set(pre_tiles) if n_pre > 0 else ())
    conv_family("wo", wo_s, wo_f, 8, 16 * 256)

    def late_conversions():
        conv_family("wgu", wgu_s, wgu_f, NFF, 16 * 256)
        conv_family("wpg", wpg_s, wpg_f, 8, 18 * 256)
        conv_family("wd", wd_s, wd_f, 20, 9 * 512)
    scr = {"fg": fg_s, "win": win_s, "wo": wo_s, "wgu": wgu_s, "wpg": wpg_s, "wd": wd_s}

    setup()

    wl = deque()
    nxt = [0]
    DEPTH_PF = NSLOT - 1

    def prefetch_upto(i):
        while nxt[0] < len(steps) and len(wl) < DEPTH_PF:
            src, _ = steps[nxt[0]]
            if src is not None:
                fam, idx, ncols, brow = src
                sap = scr[fam] if fam == "fg" else scr[fam][idx]
                wl.append((nxt[0], issue_wload(sap[:, 0:ncols], ncols, conv_bufs[fam][0 if fam == "fg" else idx], brow)))
            nxt[0] += 1

    for i, (src, fn) in enumerate(steps):
        if src is None:
            fn(None)
            prefetch_upto(i)
            continue
        prefetch_upto(i)
        if hooks:
            hooks.popleft()()
        w = None
        if src is not None:
            while wl and wl[0][0] < i:
                wl.popleft()
            assert wl and wl[0][0] == i, (i, wl[0][0] if wl else None)
            w = wl.popleft()[1]
        fn(w)
    drain()
    P.wait_all("sp", osem)

    with nc.Block() as block:
        @block.tensor
        def _(e):
            P.replay("pe", e)

        @block.scalar
        def _(e):
            P.replay("act", e)

        @block.vector
        def _(e):
            P.replay("dve", e)

        @block.gpsimd
        def _(e):
            P.replay("pool", e)

        @block.sync
        def _(e):
            P.replay("sp", e)
    stack.close()
    return nc


_NC_CACHE = {}


def kernel(**inputs):
    f = np.float32
    x = np.asarray(inputs["x"], f)
    p = np.asarray(inputs["p"], f)[0]
    sh = _prep_shared(inputs)
    if "nc" not in _NC_CACHE:
        _NC_CACHE["nc"] = build_program()
    nc = _NC_CACHE["nc"]
    in_maps = []
    for c in range(8):
        b, half = c // 2, c % 2
        d = dict(sh)
        d["x_main"] = np.ascontiguousarray(x[b, half * TOK_PER_CORE:(half + 1) * TOK_PER_CORE])
        d["x_prev"] = np.ascontiguousarray(x[b, 0:TOK_PER_CORE])
        d["p_in"] = np.ascontiguousarray(p[b, half * TOK_PER_CORE:(half + 1) * TOK_PER_CORE])
        d["flag"] = np.full((128, 1), float(half), f)
        in_maps.append(d)
    res = run_bass_kernel_spmd(nc, in_maps, core_ids=list(range(8)))
    out = np.empty((NB, SEQ, D), f)
    for c in range(8):
        b, half = c // 2, c % 2
        out[b, half * TOK_PER_CORE:(half + 1) * TOK_PER_CORE] = np.asarray(res.results[c]["out"], f)
    return out
```

```python
import numpy as np
import os
DISJ = os.environ.get('K_DISJ', '1') == '1'
from contextlib import ExitStack
from collections import deque

import concourse.bass as bass
import concourse.mybir as mybir
from concourse.bass_utils import run_bass_kernel_spmd

F32 = mybir.dt.float32
BF16 = mybir.dt.bfloat16
AF = mybir.ActivationFunctionType
ALU = mybir.AluOpType

D = 2048
SEQ = 8192
NB = 4
HEADS = 4
DK = 256
DV = 512
RANK = 16
DFF = 5632
NFF = DFF // 128
PLE = 256
DEPTH = 1
ALPHA = (2 * DEPTH) ** 0.25
EPS = 1e-5
T = 512
TOK_PER_CORE = 4096
NT = TOK_PER_CORE // T
KC = D // 128

O_Q, O_K, O_V, O_OG, O_FG, O_U, O_Z, O_GA, O_GB = 0, 1024, 2048, 4096, 6144, 6208, 8256, 10304, 12352


def _tile_k(w, col_lists):
    K = w.shape[0]
    out = []
    for cols in col_lists:
        sub = w[:, cols]
        sub = sub.reshape(K // 128, 128, len(cols)).transpose(1, 0, 2)
        out.append(sub.reshape(128, -1))
    return np.ascontiguousarray(np.stack(out, 0))


def _win_tiles():
    tiles = []
    for h in range(HEADS):
        tiles.append((("q", h), O_Q + h * 256))
        tiles.append((("k", h), O_K + h * 256))
        for j in range(2):
            tiles.append((("v", h, j), O_V + h * 512 + j * 256))
            tiles.append((("og", h, j), O_OG + h * 512 + j * 256))
            tiles.append((("ga", h, j), O_GA + h * 512 + j * 256))
    for g in range(4):
        for j in range(2):
            tiles.append((("u", g, j), O_U + g * 512 + j * 256))
            tiles.append((("gb", g, j), O_GB + g * 512 + j * 256))
            tiles.append((("z", g, j), O_Z + g * 512 + j * 256))
    return tiles


WIN_TILES = _win_tiles()
WIN_IDX = {name: i for i, (name, _) in enumerate(WIN_TILES)}
WIN_COL = {name: c for name, c in WIN_TILES}
AROW_NAMES = [n for n, _ in WIN_TILES if n[0] in ("v", "z")] + [("pg", j) for j in range(8)]
AROW_IDX = {n: i for i, n in enumerate(AROW_NAMES)}
BCOL_NAMES = []
for _n, _c in WIN_TILES:
    if _n[0] not in ("v", "z"):
        BCOL_NAMES += [(_n, 0), (_n, 1)]
BCOL_IDX = {n: i for i, n in enumerate(BCOL_NAMES)}
NBCOL = len(BCOL_NAMES)
WD_PIECES = [(0, 9), (9, 9), (18, 9), (27, 9), (36, 8)]


def _prep_shared(inp):
    f = np.float32
    w_in = np.asarray(inp["w_in"], f)[0]
    b_in = np.asarray(inp["b_in"], f)[0]
    sh = {}
    sh["win_f"] = _tile_k(w_in, [np.arange(c, c + 256) for _, c in WIN_TILES])
    sh["fg_f"] = _tile_k(w_in, [np.arange(O_FG, O_FG + 64)])[0]
    w_o = np.asarray(inp["w_o"], f)[0]
    sh["wo_f"] = _tile_k(w_o, [np.arange(j * 256, j * 256 + 256) for j in range(8)])
    w_gu = np.asarray(inp["w_gu"], f)[0]
    sh["wgu_f"] = _tile_k(w_gu, [np.concatenate([np.arange(i * 128, i * 128 + 128),
                                                  np.arange(DFF + i * 128, DFF + i * 128 + 128)])
                                 for i in range(NFF)])
    w_pg = np.asarray(inp["w_pg"], f)[0]
    w_pu = np.asarray(inp["w_pu"], f)[0]
    a = _tile_k(w_pg, [np.arange(j * 256, j * 256 + 256) for j in range(8)])
    b = _tile_k(w_pu, [np.arange(j * 256, j * 256 + 256) for j in range(8)])
    sh["wpg_f"] = np.ascontiguousarray(np.concatenate([a, b], axis=2))
    w_d = np.asarray(inp["w_down"], f)[0]
    wd = np.zeros((20, 128, 9 * 512), f)
    for cb in range(4):
        for pi, (k0, nk) in enumerate(WD_PIECES):
            sub = w_d[k0 * 128:(k0 + nk) * 128, cb * 512:(cb + 1) * 512]
            sub = sub.reshape(nk, 128, 512).transpose(1, 0, 2).reshape(128, nk * 512)
            wd[cb * 5 + pi, :, :nk * 512] = sub
    sh["wd_f"] = wd
    brow = np.zeros((len(AROW_NAMES), 256), f)
    b_pg = np.asarray(inp["b_pg"], f)[0]
    for n, i in AROW_IDX.items():
        if n[0] == "pg":
            brow[i] = b_pg[n[1] * 256:(n[1] + 1) * 256]
        else:
            brow[i] = b_in[WIN_COL[n]:WIN_COL[n] + 256]
    sh["brow_f"] = brow
    bcol = np.zeros((128, NBCOL + 1), f)
    for (n, m), i in BCOL_IDX.items():
        c0 = WIN_COL[n] + m * 128
        bcol[:, i] = b_in[c0:c0 + 128]
    bcol[:64, NBCOL] = b_in[O_FG:O_FG + 64]
    sh["bcol"] = bcol

    def cols(v, n):
        return np.ascontiguousarray(np.asarray(v, f).reshape(n, 128).T)

    small = np.concatenate([
        cols(inp["ln0_g"], 16), cols(inp["ln0_b"], 16),
        cols(np.asarray(inp["ln1_g"])[0], 16), cols(np.asarray(inp["ln1_b"])[0], 16),
        cols(np.asarray(inp["gla_norm_g"])[0], 16),
        cols(np.asarray(inp["b_f2"])[0], 8),
    ], axis=1)
    sh["small"] = np.ascontiguousarray(small)
    bc = np.stack([
        np.stack([np.asarray(inp["gmlp_ln_g"], f)[0].reshape(-1), np.asarray(inp["gmlp_ln_b"], f)[0].reshape(-1)]),
        np.stack([np.asarray(inp["ln0_g"], f), np.asarray(inp["ln0_b"], f)]),
        np.stack([np.asarray(inp["ln1_g"], f)[0], np.asarray(inp["ln1_b"], f)[0]]),
        np.stack([np.asarray(inp["ln2_g"], f)[0], np.asarray(inp["ln2_b"], f)[0]]),
    ])
    sh["bc_rows"] = np.ascontiguousarray(bc)
    wf2 = np.asarray(inp["w_f2"], f)[0]
    bd = np.zeros((64, 1024), f)
    for h in range(HEADS):
        bd[h * 16:(h + 1) * 16, h * 256:(h + 1) * 256] = wf2[h]
    sh["wf2bd"] = bd
    ws = np.asarray(inp["w_s"], f)[0]
    sh["wsT"] = np.ascontiguousarray(ws.transpose(2, 0, 1).reshape(128, 512))
    sh["bs_row"] = np.ascontiguousarray(np.asarray(inp["b_s"], f)[0].reshape(1, 512))
    sh["ident"] = np.eye(128, dtype=f)
    m = np.ones((128, 512), f)
    m[:, 0::64] = 0.0
    sh["scanmask"] = m
    return sh


class Sem:
    def __init__(self, h, name):
        self.h, self.name, self.count = h, name, 0


class Buf:
    def __init__(self, name, lo=None, hi=None):
        self.name, self.lo, self.hi = name, lo, hi
        self.w = {}
        self.r = {}
        self.alias = []
        self.valid = True


class V:
    def __init__(self, ap, bufs):
        self.ap = ap
        self.bufs = bufs if isinstance(bufs, (list, tuple)) else [bufs]

    def __getitem__(self, k):
        return V(self.ap[k], self.bufs)

    def re(self, s, **kw):
        return V(self.ap.rearrange(s, **kw), self.bufs)

    def bc(self, shape):
        return V(self.ap.broadcast_to(shape), self.bufs)

    def only(self, *idx):
        return V(self.ap, [self.bufs[i] for i in idx])


class Eng:
    def __init__(self, name, sem):
        self.name, self.sem = name, sem
        self.seen = {}
        self.ops = []


class Prog:
    def __init__(self, nc, stack):
        self.nc, self.stack = nc, stack
        self.eng = {}
        for n in ("pe", "act", "dve", "pool", "sp"):
            s = Sem(stack.enter_context(nc.semaphore("s_" + n)), n)
            self.eng[n] = Eng(n, s)
        self.nsem = 0

    def new_sem(self, name):
        self.nsem += 1
        return Sem(self.stack.enter_context(self.nc.semaphore("d_%s_%d" % (name, self.nsem))), name)

    def _deps(self, E, reads, writes, accum, disjoint=False):
        deps = {}

        def need(st):
            if st is None:
                return
            s, c = st
            if deps.get(s, 0) < c:
                deps[s] = c

        for b in reads:
            for st in b.w.items():
                need(st)
            for a in b.alias:
                for st in a.w.items():
                    need(st)
        for b in writes:
            if not disjoint:
                for st in b.w.items():
                    need(st)
            for s, c in b.r.items():
                need((s, c))
            for a in b.alias:
                for st in a.w.items():
                    need(st)
                for s, c in a.r.items():
                    need((s, c))
        for s, c in deps.items():
            if s is E.sem and E.name in ("pe", "sp"):
                continue
            if E.seen.get(s, 0) >= c:
                continue
            E.seen[s] = c
            E.ops.append(("wait", s.h, c))

    def emit(self, en, fn, reads=(), writes=(), accum=False, disjoint=False):
        E = self.eng[en]
        rb = [b for v in reads for b in v.bufs]
        wb = [b for v in writes for b in v.bufs]
        self._deps(E, rb, wb, accum, disjoint)
        E.sem.count += 1
        E.ops.append(("ins", fn, E.sem.h, 1))
        st = (E.sem, E.sem.count)
        for b in rb:
            assert b.valid, "stale read of %s (overwritten through an alias)" % b.name
            if b.r.get(E.sem, 0) < E.sem.count:
                b.r[E.sem] = E.sem.count
        for b in wb:
            if disjoint:
                b.w[st[0]] = st[1]
            else:
                b.w = {st[0]: st[1]}
                b.r = {}
            b.valid = True
            for a in b.alias:
                a.valid = False

    def dma(self, en, out, in_, sem, reads=(), writes=(), **kw):
        E = self.eng[en]
        rb = [b for v in reads for b in v.bufs]
        wb = [b for v in writes for b in v.bufs]
        self._deps(E, rb, wb, False)
        sem.count += 16
        E.ops.append(("ins", lambda e, o=out, i=in_, kw=kw: e.dma_start(out=o, in_=i, **kw), sem.h, 16))
        st = (sem, sem.count)
        for b in rb:
            assert b.valid, "stale read of %s (overwritten through an alias)" % b.name
            b.r[sem] = sem.count
        for b in wb:
            b.w = {st[0]: st[1]}
            b.r = {}
            b.valid = True
            for a in b.alias:
                a.valid = False

    def wait_all(self, en, sems):
        E = self.eng[en]
        for s in sems:
            if s.count > 0:
                E.ops.append(("wait", s.h, s.count))

    def replay(self, en, handle):
        for op in self.eng[en].ops:
            if op[0] == "wait":
                handle.wait_ge(op[1], op[2])
            else:
                ins = op[1](handle)
                ins.then_inc(op[2], op[3])


class Arena:
    def __init__(self, nc, stack, nbytes):
        self.nbytes = nbytes
        self.t = stack.enter_context(nc.sbuf_tensor("arena", [128, nbytes // 2], BF16))
        self.bufs = []

    def alloc(self, name, free, dtype, off, nsub=1, parts=128):
        esz = 4 if dtype == F32 else 2
        n = int(np.prod(free))
        nbytes = n * esz
        assert off % 4 == 0 and off + nbytes <= self.nbytes, (name, off, nbytes, self.nbytes)
        ap = self.t[0:parts, off // 2:(off + nbytes) // 2]
        if dtype == F32:
            ap = ap.bitcast(F32)
        if len(free) == 2:
            ap = ap.rearrange("p (a b) -> p a b", b=free[1])
        elif len(free) == 3:
            ap = ap.rearrange("p (a b c) -> p a b c", b=free[1], c=free[2])
        bufs = []
        sub = nbytes // nsub
        for i in range(nsub):
            b = Buf("%s%d" % (name, i), off + i * sub, off + (i + 1) * sub)
            for o in self.bufs:
                if o.lo < b.hi and b.lo < o.hi:
                    o.alias.append(b)
                    b.alias.append(o)
            self.bufs.append(b)
            bufs.append(b)
        return V(ap, bufs)


def build_program(n_pre=NT, n_main=NT, debug=False):
    nc = bass.Bass("TRN2", target_bir_lowering=False)
    stack = ExitStack()
    P = Prog(nc, stack)

    def din(name, shape, dt=F32):
        return nc.dram_tensor(name, list(shape), dt, kind="ExternalInput").ap()

    def dscr(name, shape, dt=BF16):
        return nc.dram_tensor(name, list(shape), dt, kind="Internal").ap()

    x_main = din("x_main", [n_main * T, D])
    x_prev = din("x_prev", [max(n_pre, 1) * T, D])
    p_in = din("p_in", [n_main * T, PLE])
    flag = din("flag", [128, 1])
    win_f = din("win_f", [len(WIN_TILES), 128, 16 * 256])
    fg_f = din("fg_f", [128, 16 * 64])
    wo_f = din("wo_f", [8, 128, 16 * 256])
    wgu_f = din("wgu_f", [NFF, 128, 16 * 256])
    wpg_f = din("wpg_f", [8, 128, 18 * 256])
    wd_f = din("wd_f", [20, 128, 9 * 512])
    brow_f = din("brow_f", [len(AROW_NAMES), 256])
    bcol_d = din("bcol", [128, NBCOL + 1])
    small_d = din("small", [128, 88])
    bc_rows = din("bc_rows", [4, 2, D])
    wf2bd_d = din("wf2bd", [64, 1024])
    wsT_d = din("wsT", [128, 512])
    bs_row_d = din("bs_row", [1, 512])
    ident_d = din("ident", [128, 128])
    mask_d = din("scanmask", [128, 512])
    out_d = nc.dram_tensor("out", [n_main * T, D], F32, kind="ExternalOutput").ap()

    win_s = dscr("win_s", [len(WIN_TILES), 128, 16 * 256])
    fg_s = dscr("fg_s", [128, 16 * 64])
    wo_s = dscr("wo_s", [8, 128, 16 * 256])
    wgu_s = dscr("wgu_s", [NFF, 128, 16 * 256])
    wpg_s = dscr("wpg_s", [8, 128, 18 * 256])
    wd_s = dscr("wd_s", [20, 128, 9 * 512])
    brow_s = dscr("brow_s", [len(AROW_NAMES), 256])

    KB = 1024
    off = [0]

    def take(nbytes):
        o = off[0]
        off[0] += (nbytes + 31) // 32 * 32
        return o

    ARENA = 207 * KB
    A = Arena(nc, stack, ARENA)
    WSLOT = 18 * 256
    NSLOT = 4
    wslots = [A.alloc("w%d" % i, (WSLOT,), BF16, take(WSLOT * 2)) for i in range(NSLOT)]
    wrows = [A.alloc("wr%d" % i, (256,), BF16, take(512)) for i in range(NSLOT)]
    for i in range(NSLOT):
        wrows[i].bufs = wslots[i].bufs
    wsem = [P.new_sem("w") for _ in range(NSLOT)]
    hT = A.alloc("hT", (KC, T), BF16, take(KC * T * 2))
    S32 = A.alloc("S32", (8, 512), F32, take(8 * 512 * 4), nsub=8)
    Sbf = A.alloc("Sbf", (8, 512), BF16, take(8 * 512 * 2), nsub=8)
    bcs = [A.alloc("bc%d" % i, (2, D), F32, take(2 * D * 4)) for i in range(1)]
    bcsem = [P.new_sem("bc") for _ in range(1)]
    xnA0 = A.alloc("xnA0", (D,), BF16, take(D * 2))
    pT = A.alloc("pT", (2, T), BF16, take(2 * T * 2))
    xsem = [P.new_sem("x") for _ in range(5)]
    psem = P.new_sem("p")
    bcol = A.alloc("bcol", (NBCOL + 1,), F32, take((NBCOL + 1) * 4))
    small = A.alloc("small", (88,), F32, take(88 * 4))
    nbf2 = A.alloc("nbf2", (8,), F32, take(32))
    flagc = A.alloc("flagc", (1,), F32, take(32))
    identb = A.alloc("identb", (128,), BF16, take(256))
    mask = A.alloc("mask", (512,), F32, take(2048))
    wf2b = A.alloc("wf2b", (1024,), BF16, take(2048), parts=64)
    wmT = A.alloc("wmT", (512,), BF16, take(1024))
    ones2 = A.alloc("ones2", (128,), BF16, take(256), parts=2)
    bs2 = A.alloc("bs2", (512,), BF16, take(1024), parts=2)
    mhalf = A.alloc("mhalf", (1,), F32, take(32))
    csem = P.new_sem("const")
    st6 = A.alloc("st6", (4, 4, 6), F32, take(4 * 96), nsub=4)
    mv = A.alloc("mv", (4, 2), F32, take(32), nsub=4)
    ln0s2 = [A.alloc("ln0s%d" % i, (4, 2), F32, take(32), nsub=4) for i in range(2)]
    lns = A.alloc("lns", (4, 2), F32, take(32), nsub=4)
    sq = A.alloc("sq", (2, 4), F32, take(32), nsub=2)
    dec = A.alloc("dec", (8, 8), F32, take(256))
    ovl = take(0)
    assert ovl + 98 * KB <= ARENA, (ovl, ARENA)

    def at(kb):
        return ovl + int(kb * KB)

    qsT = [A.alloc("qsT%d" % i, (2, T), BF16, at(0 + 2 * i)) for i in range(2)]
    ktT = [A.alloc("ktT%d" % i, (2, T), BF16, at(4 + 2 * i)) for i in range(2)]
    kttok = [A.alloc("ktt%d" % i, (4, 256), BF16, at(8 + 2 * i)) for i in range(2)]
    vtok = [A.alloc("vtok%d" % i, (4, 512), BF16, at(12 + 4 * i)) for i in range(2)]
    P1 = [A.alloc("P1_%d" % i, (4, T), BF16, at(20 + 4 * i)) for i in range(2)]
    actb = [A.alloc("actb%d" % i, (T,), BF16, at(56 + i)) for i in range(3)]
    xnb1 = [A.alloc("xnb1_%d" % i, (D,), BF16, at(32 + 4 * i)) for i in range(4)]
    pbuf = A.alloc("pbuf", (4, PLE), F32, at(56))
    pnb = A.alloc("pnb", (4, PLE), BF16, at(60))
    P2 = [A.alloc("P2_%d" % i, (4, T), BF16, at(32 + 4 * i)) for i in range(2)]
    gz = A.alloc("gz", (4, 512), F32, at(40), nsub=4)
    zn = [A.alloc("zn%d" % i, (4, 512), BF16, at(48)) for i in range(1)]
    tmpS = [A.alloc("tmpS%d" % i, (4, 128), F32, at(52 + 2 * i)) for i in range(2)]
    kdec = A.alloc("kdec", (8, T), BF16, at(62), nsub=8)
    tmpE = [A.alloc("tmpE%d" % i, (T,), F32, at(70 + 2 * i)) for i in range(2)]
    tmpB = [A.alloc("tmpB%d" % i, (T,), F32, at(74 + 2 * i)) for i in range(2)]
    fgT = A.alloc("fgT", (T,), BF16, at(78), parts=64)
    ohat = [A.alloc("ohat%d" % i, (512,), BF16, at(79 + i)) for i in range(2)]
    junk = A.alloc("junk", (512,), BF16, at(81))
    mT = A.alloc("mT", (KC, T), BF16, at(82), nsub=KC)
    xbufA = A.alloc("xbA", (D,), F32, at(76))
    xnA = [xnA0] + [A.alloc("xnA%d" % i, (D,), BF16, at(84 + 4 * (i - 1))) for i in range(1, 4)]
    stg = A.alloc("stg", (1024,), F32, at(0))
    resid = A.alloc("resid", (4, D), F32, at(0), nsub=4)
    actT = A.alloc("actT", (NFF, T), BF16, at(32))
    sgt = [A.alloc("sgt%d" % i, (T,), F32, at(76 + 2 * i)) for i in range(2)]
    spg = [A.alloc("spg%d" % i, (256,), F32, at(80 + i)) for i in range(2)]
    ple = [A.alloc("ple%d" % i, (256,), F32, at(82 + i)) for i in range(2)]
    pgt = [A.alloc("pgt%d" % i, (256,), F32, at(84 + i)) for i in range(2)]
    ztmp = [A.alloc("ztmp%d" % i, (256,), F32, at(59 + i)) for i in range(2)]

    banks = []
    for i in range(8):
        t = stack.enter_context(nc.psum_tensor("pb%d" % i, [128, 512], F32))
        banks.append(V(t[:], Buf("pb%d" % i)))

    def bank_bf(b):
        return V(b.ap.bitcast(BF16), b.bufs)

    def mm(out, lhsT, rhs, start, stop):
        P.emit("pe", lambda e, o=out.ap, l=lhsT.ap, r=rhs.ap, s=start, t=stop: e.matmul(o, lhsT=l, rhs=r, start=s, stop=t),
               reads=[lhsT, rhs], writes=[out], accum=not start)

    def tr(out, in_):
        P.emit("pe", lambda e, o=out.ap, i=in_.ap, d=identb.ap: e.transpose(o, i, d[0:i.shape[0], 0:i.shape[0]]),
               reads=[in_, identb], writes=[out], accum=True)

    def act(out, in_, func, bias=None, scale=None, accum_out=None, disjoint=False):
        kw = {}
        rd = [in_]
        if bias is not None:
            if isinstance(bias, V):
                kw["bias"] = bias.ap
                rd.append(bias)
            else:
                kw["bias"] = float(bias)
        if scale is not None:
            if isinstance(scale, V):
                kw["scale"] = scale.ap
                rd.append(scale)
            else:
                kw["scale"] = float(scale)
        wr = [out]
        if accum_out is not None:
            kw["accum_out"] = accum_out.ap
            wr.append(accum_out)
        P.emit("act", lambda e, o=out.ap, i=in_.ap, f=func, kw=kw: e.activation(out=o, in_=i, func=f, **kw),
               reads=rd, writes=wr, disjoint=disjoint)

    def _sc(x, rd):
        if isinstance(x, V):
            rd.append(x)
            return x.ap
        return None if x is None else float(x)

    def ts(en, out, in0, s1, s2, op0, op1=None, disjoint=False):
        rd = [in0]
        a1, a2 = _sc(s1, rd), _sc(s2, rd)
        if op1 is None:
            P.emit(en, lambda e, o=out.ap, i=in0.ap: e.tensor_scalar(out=o, in0=i, scalar1=a1, scalar2=None, op0=op0),
                   reads=rd, writes=[out])
        else:
            P.emit(en, lambda e, o=out.ap, i=in0.ap: e.tensor_scalar(out=o, in0=i, scalar1=a1, scalar2=a2, op0=op0, op1=op1),
                   reads=rd, writes=[out], disjoint=disjoint)

    def stt(out, in0, s, in1, op0, op1):
        rd = [in0, in1]
        a = _sc(s, rd)
        P.emit("dve", lambda e, o=out.ap, i0=in0.ap, i1=in1.ap: e.scalar_tensor_tensor(out=o, in0=i0, scalar=a, in1=i1, op0=op0, op1=op1),
               reads=rd, writes=[out])

    def tt(en, out, in0, in1, op):
        P.emit(en, lambda e, o=out.ap, i0=in0.ap, i1=in1.ap: e.tensor_tensor(out=o, in0=i0, in1=i1, op=op),
               reads=[in0, in1], writes=[out])

    def cp(en, out, in_):
        if en == "act":
            act(out, in_, AF.Copy)
        else:
            P.emit(en, lambda e, o=out.ap, i=in_.ap: e.tensor_copy(out=o, in_=i), reads=[in_], writes=[out])

    def memset(en, out, val):
        P.emit(en, lambda e, o=out.ap: e.memset(o, val), writes=[out])

    MUL, ADD, SUB, POW = ALU.mult, ALU.add, ALU.subtract, ALU.pow

    conv_bufs = {}

    def conv_family(name, dst, src, ntiles, width, order=None, single=()):
        sem = P.new_sem("cv_" + name)
        ch = max(c for c in range(1, 2049) if width % c == 0)
        bufs = {}
        grouped = []
        for i in (order if order is not None else range(ntiles)):
            b = Buf("%s_s%d" % (name, i))
            d_ap = dst[i] if ntiles > 1 or len(dst.shape) == 3 else dst
            s_ap = src[i] if ntiles > 1 or len(src.shape) == 3 else src
            sm = P.new_sem("cv1_" + name) if i in single else sem
            P.dma("pool", d_ap.rearrange("p (a b) -> p a b", b=ch), s_ap.rearrange("p (a b) -> p a b", b=ch), sm,
                  writes=[V(None, b)])
            bufs[i] = b
            if i not in single:
                grouped.append(b)
        for b in grouped:
            b.w = {sem: sem.count}
        conv_bufs[name] = [bufs[i] for i in range(ntiles)]

    def setup():
        cdma = []

        def ld(dst, src):
            P.dma("sp", dst.ap, src, P.new_sem("c"), writes=[dst])

        ld(bcol, bcol_d)
        ld(small, small_d)
        ld(flagc, flag)
        ld(mask, mask_d)
        ld(stg[:, 0:128], ident_d)
        cp("dve", identb, stg[:, 0:128])
        ld(stg[0:64, 0:1024], wf2bd_d)
        cp("dve", wf2b, stg[0:64, 0:1024])
        ld(stg[:, 0:512], wsT_d)
        memset("dve", stg[64:128, 0:512].re("p (g i) -> p g i", i=128)[:, :, 0:64], 0.0)
        cp("dve", wmT, stg[:, 0:512])
        ld(stg[0:1, 0:512], bs_row_d)
        cp("dve", bs2[0:1, :], stg[0:1, 0:512])
        cp("dve", stg[0:1, 512:1024], bs2[0:1, :])
        tt("dve", stg[0:1, 0:512], stg[0:1, 0:512], stg[0:1, 512:1024], SUB)
        cp("dve", junk[0:1, 0:512], stg[0:1, 0:512])
        P.dma("sp", bs2.ap[1:2, :], junk.ap[0:1, 0:512], P.new_sem("c"), reads=[junk], writes=[bs2])
        memset("dve", ones2, 1.0)
        memset("dve", mhalf, -0.5)
        ts("dve", nbf2, small[:, 80:88], -1.0, None, MUL)
        for i in range(8):
            memset("dve", S32.only(i)[:, i, :], 0.0)
            memset("dve", Sbf.only(i)[:, i, :], 0.0)

    wstate = {"n": 0}

    class WLoad:
        pass

    def issue_wload(src_ap, ncols_total, srcbuf, brow_idx=None):
        i = wstate["n"] % NSLOT
        wstate["n"] += 1
        sl = wslots[i]
        P.dma("sp", sl.ap[:, 0:ncols_total], src_ap, wsem[i], reads=[V(None, srcbuf)], writes=[sl])
        if brow_idx is not None:
            rsrc = brow_s[brow_idx:brow_idx + 1, :]
            rsrc = bass.AP(rsrc.tensor, rsrc.offset, [[0, 128], [1, 256]])
            P.dma("sp", wrows[i].ap, rsrc, wsem[i],
                  reads=[V(None, conv_bufs["brow"][0])], writes=[sl])
        w = WLoad()
        w.v = sl
        w.row = wrows[i]
        return w

    bg = deque()

    def pump(n=1):
        for _ in range(n):
            if bg:
                bg.popleft()()

    def drain():
        while bg:
            bg.popleft()()

    def keep(n):
        while len(bg) > n:
            bg.popleft()()

    bcstate = {"n": 0}

    def load_bc(which):
        i = 0
        src = bc_rows[which]
        src_b = bass.AP(src.tensor, src.offset, [[0, 128], [D, 2], [1, D]])
        P.dma("pool", bcs[i].ap, src_b, bcsem[i], writes=[bcs[i]])
        return bcs[i]

    def rstd_from(var_v, out_rstd, tmp):
        ts("pool", tmp, var_v, EPS, None, ADD)
        tt("pool", out_rstd, tmp, mhalf, POW)

    def ln_stats(src, rstd_out, nmr_out, nchunk=4, width=512, tb=0):
        s6 = st6.only(tb)[:, tb, :, :]
        m2 = mv.only(tb)[:, tb, :]
        for c in range(nchunk):
            P.emit("dve", lambda e, o=s6.ap[:, c, :], i=src.ap[:, c * width:(c + 1) * width]: e.bn_stats(out=o, in_=i),
                   reads=[src], writes=[s6])
        P.emit("dve", lambda e, o=m2.ap, i=s6.ap[:, 0:nchunk, :].rearrange("p a b -> p (a b)"): e.bn_aggr(out=o, in_=i),
               reads=[s6], writes=[m2])
        rstd_from(m2[:, 1:2], rstd_out, m2[:, 1:2])
        stt(nmr_out, m2[:, 0:1], -1.0, rstd_out, MUL, MUL)

    xstate = {"n": 0}


    def transposes_to_fm(src_bf, dstT, tb, gcol0, bcol0, ptb):
        for r in range(2):
            ptv = bank_bf(banks[2 + r])
            for j in range(8):
                k = r * 8 + j
                tr(ptv[:, j * 128:(j + 1) * 128], src_bf[:, k * 128:(k + 1) * 128])
            for j in range(8):
                k = r * 8 + j
                eng = "dve" if (r % 2 == 0 or os.environ.get('K_EV') == 'dve') else "act"
                if eng == "dve":
                    ts("dve", dstT[:, k, tb * 128:(tb + 1) * 128], ptv[:, j * 128:(j + 1) * 128],
                       small[:, gcol0 + k:gcol0 + k + 1], small[:, bcol0 + k:bcol0 + k + 1], MUL, ADD, disjoint=DISJ)
                else:
                    act(dstT[:, k, tb * 128:(tb + 1) * 128], ptv[:, j * 128:(j + 1) * 128], AF.Identity,
                        bias=small[:, bcol0 + k:bcol0 + k + 1], scale=small[:, gcol0 + k:gcol0 + k + 1], disjoint=DISJ)

    lnpar = {"n": 0}

    hooks = deque()

    def stage_ln0_pre(xd, t, now=False):
        lnpar["n"] += 1
        l0 = ln0s2[lnpar["n"] % 2]

        def load(tb):
            r0 = t * T + tb * 128
            P.dma("act", xbufA.ap, xd[r0:r0 + 128, :], xsem[0], writes=[xbufA])

        def comp(tb):
            lv = l0.only(tb)[:, tb, :]
            ln_stats(xbufA, lv[:, 0:1], lv[:, 1:2], tb=tb)
            act(xnA[tb], xbufA, AF.Identity, bias=lv[:, 1:2], scale=lv[:, 0:1])

        pieces = [lambda: load(0)]
        for tb in range(4):
            pieces.append(lambda tb=tb: (comp(tb), load(tb + 1) if tb < 3 else None))
        if now:
            for p_ in pieces:
                p_()
        else:
            hooks.extend(pieces)

    def run_hooks():
        while hooks:
            hooks.popleft()()

    def stage_ln0(xd, t):
        run_hooks()
        for tb in range(4):
            transposes_to_fm(xnA[tb], hT, tb, 0, 16, banks[2])
            pump(3)
        drain()

    def cur_ln0s():
        return ln0s2[lnpar["n"] % 2]

    def stage_p(t):
        P.dma("sp", pbuf.ap, p_in[t * T:(t + 1) * T, :].rearrange("(a p) c -> p a c", p=128), psem, writes=[pbuf])
        cp("pool", pnb, pbuf)
        ptv = bank_bf(banks[2])
        for tb in range(4):
            for k2 in range(2):
                tr(ptv[:, (tb * 2 + k2) * 128:(tb * 2 + k2 + 1) * 128], pnb[:, tb, k2 * 128:(k2 + 1) * 128])
        for k2 in range(2):
            cp("dve", pT[:, k2, :].re("p (a b) -> p a b", b=128),
               ptv.re("p (a k b) -> p a k b", k=2, b=128)[:, :, k2, :])

    def proj_B(w, nk, m_off, M, rhsT, bank, N=T, wcols=256):
        wv = w.v.re("p (k c) -> p k c", c=wcols) if not hasattr(w, "re3") else w.re3
        for k in range(nk):
            mm(bank[0:M, 0:N], wv[:, k, m_off:m_off + M], rhsT[:, k, 0:N], k == 0, k == nk - 1)

    def proj_A(w, nk, srcT, tb, bank, ncol=256, bias=False, k_off=0, last=True, first=True):
        wv = w.v.re("p (k c) -> p k c", c=ncol)
        for k in range(nk):
            mm(bank[:, 0:ncol], srcT[:, k, tb * 128:(tb + 1) * 128], wv[:, k_off + k, :], first and k == 0,
               last and (not bias) and k == nk - 1)
        if bias:
            mm(bank[:, 0:ncol], ones2[0:1, :], w.row[0:1, 0:ncol], False, last)

    pbrot = {"n": 0}

    def next_bank(lo=0, n=2):
        b = banks[lo + pbrot["n"] % n]
        pbrot["n"] += 1
        return b

    def stage_fg_decay(w):
        b = next_bank()
        wv = w.v.re("p (k c) -> p k c", c=64)
        for k in range(KC):
            mm(b[0:64, 0:T], wv[:, k, 0:64], hT[:, k, :], k == 0, k == KC - 1)
        act(fgT, b[0:64, 0:T], AF.Identity, bias=bcol[0:64, NBCOL:NBCOL + 1])

    def decay_hooks(delay):
        def partA(d8):
            pm = banks[4 + d8 % 4]
            mm(pm[:, 0:T], wf2b[0:64, d8 * 128:(d8 + 1) * 128], fgT[0:64, :], True, True)
            e, bb = tmpE[d8 % 2], tmpB[d8 % 2]
            act(e, pm[:, 0:T], AF.Exp, bias=nbf2[:, d8:d8 + 1], scale=-1.0)
            act(e, e, AF.Ln, bias=1.0)
            P.emit("dve", lambda en, o=bb.ap, d0=mask.ap, d1=e.ap: en.tensor_tensor_scan(out=o, data0=d0, data1=d1, initial=0.0, op0=MUL, op1=ADD),
                   reads=[mask, e], writes=[bb])

        def partB(d8):
            e, bb = tmpE[d8 % 2], tmpB[d8 % 2]
            b3 = bb.re("p (c i) -> p c i", i=64)
            act(dec[:, d8, :], b3[:, :, 63], AF.Exp, scale=-1.0 / 16.0)
            tt("dve", e.re("p (c i) -> p c i", i=64), b3, b3[:, :, 63:64].bc([128, 8, 64]), SUB)
            act(kdec.only(d8)[:, d8, :], e, AF.Exp, scale=1.0 / 16.0)

        hooks.extend([lambda: None] * delay)
        hooks.append(lambda: (partA(0), partA(1)))
        hooks.append(lambda: (partB(0), partB(1), partA(2), partA(3)))
        hooks.append(lambda: (partB(2), partB(3), partA(4), partA(5)))
        hooks.append(lambda: (partB(4), partB(5), partA(6), partA(7)))
        hooks.append(lambda: (partB(6), partB(7)))

    def stage_q(w, h):
        hb = h % 2
        for m in range(2):
            b = next_bank()
            proj_B(w, KC, m * 128, 128, hT, b)
            ci = BCOL_IDX[(("q", h), m)]
            ts("dve", qsT[hb][:, m, :], b[:, 0:T], bcol[:, ci:ci + 1], 1.0 / 16.0, ADD, MUL)
            pump()

    def stage_k(w, h):
        hb = h % 2
        for m in range(2):
            b = next_bank()
            proj_B(w, KC, m * 128, 128, hT, b)
            ci = BCOL_IDX[(("k", h), m)]
            stt(ktT[hb][:, m, :], b[:, 0:T], bcol[:, ci:ci + 1], kdec.only(h * 2 + m)[:, h * 2 + m, :], ADD, MUL)
            pump()

    def stage_v(w, h, j):
        hb = h % 2
        for tb in range(4):
            b = next_bank()
            proj_A(w, KC, hT, tb, b)
            tt("dve", vtok[hb][:, tb, j * 256:(j + 1) * 256], b[:, 0:256], w.row, ADD)
            pump()

    def gla_tasks(h, with_o):
        hb = h % 2
        ptv = bank_bf(banks[2])

        def k_transposes():
            for tb in range(4):
                for dc in range(2):
                    tr(ptv[:, (tb * 2 + dc) * 128:(tb * 2 + dc + 1) * 128], ktT[hb][:, dc, tb * 128:(tb + 1) * 128])
            cp("dve", kttok[hb].re("p a b -> p (a b)"), ptv[:, 0:1024])

        def dS(c):
            tb, pr = c // 2, c % 2
            rows = slice(pr * 64, pr * 64 + 64)
            for dc in range(2):
                i8 = h * 2 + dc
                pd = banks[4 + dc]
                mm(pd[:, 0:512], kttok[hb][rows, tb, dc * 128:(dc + 1) * 128], vtok[hb][rows, tb, :], True, True)
                sv = S32.only(i8)[:, i8, :]
                stt(sv, sv, dec[:, i8, c:c + 1], pd[:, 0:512], MUL, ADD)
                if with_o:
                    cp("pool", Sbf.only(i8)[:, i8, :], sv)

        def o_mm(c):
            tb, pr = c // 2, c % 2
            rows = slice(pr * 64, pr * 64 + 64)
            po = banks[6 + tb % 2]
            for dc in range(2):
                i8 = h * 2 + dc
                mm(po[rows, 0:512], qsT[hb][:, dc, c * 64:(c + 1) * 64], Sbf.only(i8)[:, i8, :], dc == 0, dc == 1)

        def opost_a(tb):
            po = banks[6 + tb % 2]
            s = sq.only(tb % 2)[:, tb % 2, :]
            act(junk, po[:, 0:512], AF.Square, accum_out=s[:, 0:1])
            ts("pool", s[:, 1:2], s[:, 0:1], 1.0 / 512.0, EPS, MUL, ADD)
            tt("pool", s[:, 2:3], s[:, 1:2], mhalf, POW)
            oh = ohat[tb % 2]
            act(oh, po[:, 0:512], AF.Identity, scale=s[:, 2:3])

        def opost_b(tb):
            oh = ohat[tb % 2]
            for cc in range(4):
                tr(ptv[:, cc * 128:(cc + 1) * 128], oh[:, cc * 128:(cc + 1) * 128])
            for cc in range(4):
                c16 = h * 4 + cc
                stt(mT.only(c16)[:, c16, tb * 128:(tb + 1) * 128], ptv[:, cc * 128:(cc + 1) * 128],
                    small[:, 64 + c16:65 + c16], P1[hb][:, cc, tb * 128:(tb + 1) * 128], MUL, MUL)

        noop = lambda: None
        tasks = [k_transposes, noop]
        if not with_o:
            for c in range(8):
                tasks.append(lambda c=c: dS(c))
            return tasks
        tasks.append(lambda: dS(0))
        for c in range(8):
            def step(c=c):
                o_mm(c)
                if c + 1 < 8:
                    dS(c + 1)
                if c % 2 == 1:
                    opost_a(c // 2)
                if c % 2 == 0 and c >= 2:
                    opost_b(c // 2 - 1)
            tasks.append(step)
        tasks.append(noop)
        tasks.append(lambda: opost_b(3))
        return tasks

    def stage_og_ga(w, h, j, kind):
        hb = h % 2
        for m in range(2):
            b = next_bank()
            proj_B(w, KC, m * 128, 128, hT, b)
            ci = BCOL_IDX[((kind, h, j), m)]
            cc = j * 2 + m
            if kind == "og":
                act(P1[hb][:, cc, :], b[:, 0:T], AF.Silu, bias=bcol[:, ci:ci + 1])
            else:
                a = actb[pbrot["n"] % 3]
                act(a, b[:, 0:T], AF.Sigmoid, bias=bcol[:, ci:ci + 1])
                tt("dve", P1[hb][:, cc, :], P1[hb][:, cc, :], a, MUL)
            pump()

    def stage_u_gb(w, g, j, kind):
        gb_ = g % 2
        for m in range(2):
            b = next_bank()
            proj_B(w, KC, m * 128, 128, hT, b)
            ci = BCOL_IDX[((kind, g, j), m)]
            cc = j * 2 + m
            if kind == "u":
                act(P2[gb_][:, cc, :], b[:, 0:T], AF.Gelu, bias=bcol[:, ci:ci + 1])
            else:
                a = actb[pbrot["n"] % 3]
                act(a, b[:, 0:T], AF.Sigmoid, bias=bcol[:, ci:ci + 1])
                tt("dve", P2[gb_][:, cc, :], P2[gb_][:, cc, :], a, MUL)
            pump()

    def stage_z(w, g, j, bcz):
        gb_ = g % 2
        for tb in range(4):
            b = next_bank()
            proj_A(w, KC, hT, tb, b)
            zt = ztmp[tb % 2]
            tt("dve", zt, b[:, 0:256], w.row, ADD)
            act(gz.only(tb)[:, tb, j * 256:(j + 1) * 256], zt, AF.Gelu)
            pump()
        if j == 1:
            drain()
            for tb in range(4):
                gv = gz.only(tb)[:, tb, :]
                l = lns.only(tb)[:, tb, :]
                ln_stats(gv, l[:, 0:1], l[:, 1:2], nchunk=1, tb=tb)
            for tb in range(4):
                gv = gz.only(tb)[:, tb, :]
                l = lns.only(tb)[:, tb, :]
                act(gv, gv, AF.Identity, bias=l[:, 1:2], scale=l[:, 0:1])
                tt("pool", gv, gv, bcz[:, 0, g * 512:(g + 1) * 512], MUL)
                tt("pool", zn[0][:, tb, :], gv, bcz[:, 1, g * 512:(g + 1) * 512], ADD)

    def spatial_tasks(g):
        gb_ = g % 2
        if True:
            def spatial(tb):
                pg = banks[3]
                for cc in range(4):
                    mm(pg[:, cc * 128:(cc + 1) * 128], zn[0][:, tb, cc * 128:(cc + 1) * 128],
                       wmT[:, g * 128:(g + 1) * 128], True, False)
                    mm(pg[:, cc * 128:(cc + 1) * 128], ones2[0:2, :], bs2[0:2, g * 128:(g + 1) * 128], False, True)
                tsv = tmpS[tb % 2]
                tt("dve", tsv, pg[:, 0:512].re("p (c i) -> p c i", i=128),
                   P2[gb_][:, :, tb * 128:(tb + 1) * 128], MUL)
                idx = list(range(g * 4, g * 4 + 4))
                mv_ = mT.only(*idx)[:, g * 4:(g + 1) * 4, tb * 128:(tb + 1) * 128]
                tt("dve", mv_, mv_, tsv, ADD)

            for tb in range(4):
                bg.append(lambda tb=tb: spatial(tb))
                bg.append(lambda: None)

    def stage_resid_dma(xd, t):
        for tb in range(4):
            r0 = t * T + tb * 128
            rv = resid.only(tb)[:, tb, :]
            P.dma("sp", rv.ap, xd[r0:r0 + 128, :], xsem[1 + tb], writes=[rv])

    def stage_resid_h(xd, t, bc0):
        l0 = cur_ln0s()
        bc0 = bc0()
        for tb in range(4):
            rv = resid.only(tb)[:, tb, :]
            lv = l0.only(tb)[:, tb, :]
            act(rv, rv, AF.Identity, bias=lv[:, 1:2], scale=lv[:, 0:1])
            tt("pool", rv, rv, bc0[:, 0, :], MUL)
            tt("pool", rv, rv, bc0[:, 1, :], ADD)

    def stage_wo(w, j):
        for tb in range(4):
            b = next_bank()
            wv = w.v.re("p (k c) -> p k c", c=256)
            for k in range(KC):
                mm(b[:, 0:256], mT.only(k)[:, k, tb * 128:(tb + 1) * 128], wv[:, k, :], k == 0, k == KC - 1)
            rv = resid.only(tb)[:, tb, j * 256:(j + 1) * 256]
            stt(rv, rv, ALPHA, b[:, 0:256], MUL, ADD)
            if j == 7:
                rf = resid.only(tb)[:, tb, :]
                l = lns.only(tb)[:, tb, :]
                ln_stats(rf, l[:, 0:1], l[:, 1:2], tb=tb)

    def stage_ln1(bc1):
        for tb in range(4):
            rv = resid.only(tb)[:, tb, :]
            l = lns.only(tb)[:, tb, :]
            act(xnb1[tb], rv, AF.Identity, bias=l[:, 1:2], scale=l[:, 0:1])
        for tb in range(4):
            transposes_to_fm(xnb1[tb], hT, tb, 32, 48, banks[2])
        bc1 = bc1()
        for tb in range(4):
            rv = resid.only(tb)[:, tb, :]
            l = lns.only(tb)[:, tb, :]
            act(rv, rv, AF.Identity, bias=l[:, 1:2], scale=l[:, 0:1])
            tt("pool", rv, rv, bc1[:, 0, :], MUL)
            tt("pool", rv, rv, bc1[:, 1, :], ADD)

    def stage_gu(w, f):
        bgk = banks[(f % 2) * 2]
        buk = banks[(f % 2) * 2 + 1]
        proj_B(w, KC, 0, 128, hT, bgk)
        proj_B(w, KC, 128, 128, hT, buk)
        s = sgt[f % 2]
        act(s, bgk[:, 0:T], AF.Silu)
        tt("dve", actT[:, f, :], s, buk[:, 0:T], MUL)

    def stage_pg(w, j):
        wv = w.v.re("p (k c) -> p k c", c=256)
        for tb in range(4):
            bp = banks[4 + (tb % 2) * 2]
            bq = banks[5 + (tb % 2) * 2]
            proj_A(w, KC, hT, tb, bp)
            for k2 in range(2):
                mm(bq[:, 0:256], pT[:, k2, tb * 128:(tb + 1) * 128], wv[:, 16 + k2, :], k2 == 0, k2 == 1)
            s = spg[tb % 2]
            pt_ = pgt[tb % 2]
            tt("dve", pt_, bp[:, 0:256], w.row, ADD)
            act(s, pt_, AF.Sigmoid)
            pl = ple[tb % 2]
            tt("dve", pl, s, bq[:, 0:256], MUL)
            rv = resid.only(tb)[:, tb, j * 256:(j + 1) * 256]
            stt(rv, rv, ALPHA, pl, MUL, ADD)

    def stage_wd(w, cb, pi):
        k0, nk = WD_PIECES[pi]
        wv = w.v.re("p (k c) -> p k c", c=512)
        for tb in range(4):
            b = banks[(cb % 2) * 4 + tb]
            for kk in range(nk):
                f = k0 + kk
                mm(b[:, 0:512], actT[:, f, tb * 128:(tb + 1) * 128], wv[:, kk, :], f == 0, f == NFF - 1)
        if pi == len(WD_PIECES) - 1:
            for tb in range(4):
                b = banks[(cb % 2) * 4 + tb]
                rv = resid.only(tb)[:, tb, cb * 512:(cb + 1) * 512]
                tt("dve", rv, b[:, 0:512], rv, ADD)

    osem = [P.new_sem("o") for _ in range(4)]

    def stage_ln2(t, bc2, tbs):
        for tb in tbs:
            rv = resid.only(tb)[:, tb, :]
            l = lns.only(tb)[:, tb, :]
            ln_stats(rv, l[:, 0:1], l[:, 1:2], tb=tb)
        for tb in tbs:
            rv = resid.only(tb)[:, tb, :]
            l = lns.only(tb)[:, tb, :]
            ts("dve", rv, rv, l[:, 0:1], l[:, 1:2], MUL, ADD)
            tt("pool", rv, rv, bc2[:, 0, :], MUL)
            tt("pool", rv, rv, bc2[:, 1, :], ADD)
            r0 = t * T + tb * 128
            P.dma("pool", out_d[r0:r0 + 128, :], rv.ap, osem[tb], reads=[rv])

    def stage_flag():
        for i8 in range(8):
            sv = S32.only(i8)[:, i8, :]
            ts("dve", sv, sv, flagc[:, 0:1], None, MUL)
            cp("pool", Sbf.only(i8)[:, i8, :], sv)

    steps = []

    def wsrc(fam, idx, ncols, brow=None):
        return (fam, idx, ncols, brow)

    def add(src, fn):
        steps.append((src, fn))

    def win(name, brow=False):
        return wsrc("win", WIN_IDX[name], 16 * 256, AROW_IDX[name] if brow else None)

    tiles = [("pre", t) for t in range(n_pre)] + [("main", t) for t in range(n_main)]

    def pre_of(i, now=False):
        kind, t = tiles[i]
        xd = x_prev if kind == "pre" else x_main
        return lambda w: stage_ln0_pre(xd, t, now)

    def head_of(i):
        kind, t = tiles[i]
        if kind == "pre":
            return lambda w: stage_ln0(x_prev, t)
        return lambda w: (stage_ln0(x_main, t), stage_p(t))

    pend_ln2 = []
    add(None, pre_of(0, True))
    add(None, head_of(0))
    for ti, (kind, t) in enumerate(tiles):
        last_pre = kind == "pre" and (ti + 1 == len(tiles) or tiles[ti + 1][0] != "pre")
        first_main = kind == "main" and (ti == 0 or tiles[ti - 1][0] == "pre")
        if first_main:
            add(None, lambda w: late_conversions())
        if kind == "main" and pend_ln2:
            pt_, pst_ = pend_ln2.pop()
            add(wsrc("fg", 0, 16 * 64), lambda w, pt_=pt_, pst_=pst_: (
                stage_fg_decay(w),
                hooks.append(lambda: stage_ln2(pt_, pst_["bc2"], (1,))),
                hooks.append(lambda: stage_ln2(pt_, pst_["bc2"], (3,))),
                decay_hooks(0)))
        else:
            add(wsrc("fg", 0, 16 * 64), lambda w, kind=kind: (stage_fg_decay(w), decay_hooks(0 if kind == "pre" else 2)))
        if kind == "pre":
            if ti + 1 < len(tiles):
                add(None, pre_of(ti + 1))
            for h in range(HEADS):
                for j in range(2):
                    add(win(("v", h, j), True), lambda w, h=h, j=j: ((keep(10) if j == 0 else None), stage_v(w, h, j)))
                add(win(("k", h)), lambda w, h=h: stage_k(w, h))
                add(None, lambda w, h=h: bg.extend(gla_tasks(h, False)))
            if ti + 1 < len(tiles):
                add(None, head_of(ti + 1))
            if last_pre:
                add(None, lambda w: (drain(), stage_flag()))
            continue
        st = {}
        for h in range(HEADS):
            for j in range(2):
                add(win(("og", h, j)), lambda w, h=h, j=j: stage_og_ga(w, h, j, "og"))
            for j in range(2):
                add(win(("ga", h, j)), lambda w, h=h, j=j: stage_og_ga(w, h, j, "ga"))
            add(win(("q", h)), lambda w, h=h: stage_q(w, h))
            add(win(("k", h)), lambda w, h=h: stage_k(w, h))
            for j in range(2):
                add(win(("v", h, j), True), lambda w, h=h, j=j: stage_v(w, h, j))
            add(None, lambda w, h=h: (drain(), bg.extend(gla_tasks(h, True))))
        add(None, lambda w, st=st: st.__setitem__("bcz", load_bc(0)))
        for g in range(4):
            if g == 3:
                add(None, lambda w, t=t: (drain(), stage_resid_dma(x_main, t)))
            for j in range(2):
                add(win(("z", g, j), True), lambda w, g=g, j=j, st=st: stage_z(w, g, j, st["bcz"]))
            if g == 3:
                add(None, lambda w, t=t, st=st: stage_resid_h(x_main, t, lambda: load_bc(1)))
            for j in range(2):
                add(win(("u", g, j)), lambda w, g=g, j=j: stage_u_gb(w, g, j, "u"))
            for j in range(2):
                add(win(("gb", g, j)), lambda w, g=g, j=j: stage_u_gb(w, g, j, "gb"))
            add(None, lambda w, g=g: spatial_tasks(g))
        add(None, lambda w: drain())
        for j in range(8):
            add(wsrc("wo", j, 16 * 256), lambda w, j=j: stage_wo(w, j))
        add(None, lambda w, st=st: stage_ln1(lambda: load_bc(2)))
        for f in range(NFF):
            add(wsrc("wgu", f, 16 * 256), lambda w, f=f: stage_gu(w, f))
        add(None, lambda w, st=st: st.__setitem__("bc2", load_bc(3)))
        for j in range(8):
            add(wsrc("wpg", j, 18 * 256, AROW_IDX[("pg", j)]), lambda w, j=j: stage_pg(w, j))
        if ti + 1 < len(tiles):
            add(None, pre_of(ti + 1))
        for cb in range(4):
            for pi in range(len(WD_PIECES)):
                add(wsrc("wd", cb * 5 + pi, 9 * 512), lambda w, cb=cb, pi=pi: stage_wd(w, cb, pi))
        add(None, lambda w, t=t, st=st: stage_ln2(t, st["bc2"], (2,)))
        if ti + 1 < len(tiles):
            add(None, head_of(ti + 1))
            add(None, lambda w, t=t, st=st: stage_ln2(t, st["bc2"], (0,)))
            pend_ln2.append((t, st))
        else:
            add(None, lambda w, t=t, st=st: stage_ln2(t, st["bc2"], (0, 1, 3)))

    conv_family("fg", fg_s, fg_f, 1, 16 * 64)
    conv_family("brow", brow_s.rearrange("(o a) b -> o a b", o=1), brow_f.rearrange("(o a) b -> o a b", o=1), 1, 256)
    pre_tiles = []
    for h in range(HEADS):
        pre_tiles += [WIN_IDX[("v", h, 0)], WIN_IDX[("v", h, 1)], WIN_IDX[("k", h)]]
    main_order = []
    for h in range(HEADS):
        main_order += [WIN_IDX[(k_, h, j)] for k_ in ("og", "ga") for j in range(2)] + [WIN_IDX[("q", h)]]
    for g in range(4):
        main_order += [WIN_IDX[(k_, g, j)] for k_ in ("z", "u", "gb") for j in range(2)]
    conv_family("win", win_s, win_f, len(WIN_TILES), 16 * 256, order=pre_tiles + main_order,
                single=set(pre_tiles) if n_pre > 0 else ())
    conv_family("wo", wo_s, wo_f, 8, 16 * 256)

    def late_conversions():
        conv_family("wgu", wgu_s, wgu_f, NFF, 16 * 256)
        conv_family("wpg", wpg_s, wpg_f, 8, 18 * 256)
        conv_family("wd", wd_s, wd_f, 20, 9 * 512)
    scr = {"fg": fg_s, "win": win_s, "wo": wo_s, "wgu": wgu_s, "wpg": wpg_s, "wd": wd_s}

    setup()

    wl = deque()
    nxt = [0]
    DEPTH_PF = NSLOT - 1

    def prefetch_upto(i):
        while nxt[0] < len(steps) and len(wl) < DEPTH_PF:
            src, _ = steps[nxt[0]]
            if src is not None:
                fam, idx, ncols, brow = src
                sap = scr[fam] if fam == "fg" else scr[fam][idx]
                wl.append((nxt[0], issue_wload(sap[:, 0:ncols], ncols, conv_bufs[fam][0 if fam == "fg" else idx], brow)))
            nxt[0] += 1

    for i, (src, fn) in enumerate(steps):
        if src is None:
            fn(None)
            prefetch_upto(i)
            continue
        prefetch_upto(i)
        if hooks:
            hooks.popleft()()
        w = None
        if src is not None:
            while wl and wl[0][0] < i:
                wl.popleft()
            assert wl and wl[0][0] == i, (i, wl[0][0] if wl else None)
            w = wl.popleft()[1]
        fn(w)
    drain()
    P.wait_all("sp", osem)

    with nc.Block() as block:
        @block.tensor
        def _(e):
            P.replay("pe", e)

        @block.scalar
        def _(e):
            P.replay("act", e)

        @block.vector
        def _(e):
            P.replay("dve", e)

        @block.gpsimd
        def _(e):
            P.replay("pool", e)

        @block.sync
        def _(e):
            P.replay("sp", e)
    stack.close()
    return nc


_NC_CACHE = {}


def kernel(**inputs):
    f = np.float32
    x = np.asarray(inputs["x"], f)
    p = np.asarray(inputs["p"], f)[0]
    sh = _prep_shared(inputs)
    if "nc" not in _NC_CACHE:
        _NC_CACHE["nc"] = build_program()
    nc = _NC_CACHE["nc"]
    in_maps = []
    for c in range(8):
        b, half = c // 2, c % 2
        d = dict(sh)
        d["x_main"] = np.ascontiguousarray(x[b, half * TOK_PER_CORE:(half + 1) * TOK_PER_CORE])
        d["x_prev"] = np.ascontiguousarray(x[b, 0:TOK_PER_CORE])
        d["p_in"] = np.ascontiguousarray(p[b, half * TOK_PER_CORE:(half + 1) * TOK_PER_CORE])
        d["flag"] = np.full((128, 1), float(half), f)
        in_maps.append(d)
    res = run_bass_kernel_spmd(nc, in_maps, core_ids=list(range(8)))
    out = np.empty((NB, SEQ, D), f)
    for c in range(8):
        b, half = c // 2, c % 2
        out[b, half * TOK_PER_CORE:(half + 1) * TOK_PER_CORE] = np.asarray(res.results[c]["out"], f)
    return out
```
